# Optimizing a Trainium2 kernel written in Bass

```python
import math
import jax
import jax.numpy as jnp
from jax import lax
import numpy as np

D_MODEL = 2048
BATCH = 16
SEQ = 256
DEPTH = 4
DEC_BATCH = 8
DEC_SEQ = 4096
PAST_LEN = 256

GRID_W = 64
N_MIXERS = 3
N_A = (DEPTH + 2) // 3
N_B = (DEPTH + 1) // 3
N_C = DEPTH // 3
ALPHA = (2 * DEPTH) ** 0.25
BETA = (8 * DEPTH) ** -0.25
LN_EPS = 1e-5
D_RNN = D_MODEL
RG_BLOCKS = 8
RG_BS = D_RNN // RG_BLOCKS
RG_C = 8.0
CONV_W = 4
CONV_LEFT = 2
D_S5 = D_MODEL
S5_K = 16
S5_G = D_S5 // S5_K
S5_P = 64
SCAN_CHUNK = 128
DA_HEADS = 8
DA_DH = D_MODEL // (2 * DA_HEADS)
ROPE_BASE = 10000.0
Q_BLOCK = 128
PEER_HEADS = 8
PEER_NK = 128
PEER_N = PEER_NK * PEER_NK
PEER_DK = 256
PEER_DKH = PEER_DK // 2
PEER_TOPK = 16
PEER_BLOCK = 128

kernel_name = 'hybrid_rglru_s5_diffattn_peer_diffusion_step'

F32 = jnp.float32


def _layer_norm(x, g, b):
    xf = x.astype(F32)
    mu = jnp.mean(xf, axis=-1, keepdims=True)
    var = jnp.mean(jnp.square(xf - mu), axis=-1, keepdims=True)
    return (xf - mu) * lax.rsqrt(var + LN_EPS) * g.astype(F32) + b.astype(F32)


def _post_norm(x, gate, y, g, b):
    z = ALPHA * x.astype(F32) + gate.astype(F32) * y.astype(F32)
    return _layer_norm(z, g, b).astype(x.dtype)


def _modulation(cond, w, b):
    m = jax.nn.silu(cond) @ w + b
    return m.reshape(cond.shape[0], 6, 1, D_MODEL)


def _modulate(x, shift, scale):
    return x * (1 + scale) + shift


def _real_comb(e1, e2):
    a1, b1 = e1
    a2, b2 = e2
    return a1 * a2, a2 * b1 + b2


def _linear_scan(a, b, h0, reverse):
    A, Bc = lax.associative_scan(_real_comb, (a, b), reverse=reverse, axis=1)
    return A * h0[:, None] + Bc


def _cplx_comb(e1, e2):
    a1r, a1i, b1r, b1i = e1
    a2r, a2i, b2r, b2i = e2
    return (a1r * a2r - a1i * a2i, a1r * a2i + a1i * a2r,
            a2r * b1r - a2i * b1i + b2r, a2r * b1i + a2i * b1r + b2i)


def _complex_linear_scan(ar, ai, br, bi, h0r, h0i, reverse):
    Ar, Ai, Br, Bi = lax.associative_scan(_cplx_comb, (ar, ai, br, bi), reverse=reverse, axis=1)
    h0r, h0i = h0r[:, None], h0i[:, None]
    return Ar * h0r - Ai * h0i + Br, Ar * h0i + Ai * h0r + Bi


def _centred_dwconv(x, w, b):
    y = lax.conv_general_dilated(x, w[:, None, :], window_strides=(1,),
                                 padding=[(CONV_LEFT, CONV_W - 1 - CONV_LEFT)],
                                 dimension_numbers=('NWC', 'WIO', 'NWC'),
                                 feature_group_count=x.shape[-1])
    return y + b


def _rglru_mixer(h, h0, w_in, conv_w, conv_b, gate_w, gate_b, lam, w_out):
    b_, l_, _ = h.shape
    gate_branch, xr = jnp.split(h @ w_in, 2, axis=-1)
    xc = _centred_dwconv(xr, conv_w, conv_b)
    xg = xc.reshape(b_, l_, RG_BLOCKS, RG_BS)
    xcf = xc.astype(F32)
    h0 = h0.astype(F32)
    hs = []
    for d in range(2):
        gts = jnp.einsum('blnk,gnkj->gblnj', xg, gate_w[d]).reshape(2, b_, l_, D_RNN)
        gts = gts.astype(F32) + gate_b[d].astype(F32)[:, None, None, :]
        r = jax.nn.sigmoid(gts[0])
        i_g = jax.nn.sigmoid(gts[1])
        log_a = -RG_C * r * jax.nn.softplus(-lam[d].astype(F32))
        a = jnp.exp(log_a)
        bb = jnp.sqrt(-jnp.expm1(2.0 * log_a)) * i_g * xcf
        hs.append(_linear_scan(a, bb, h0[:, d], reverse=(d == 1)))
    y = ((hs[0] + hs[1]) * jax.nn.gelu(gate_branch.astype(F32))).astype(h.dtype)
    state = jnp.stack([hs[0][:, -1], hs[1][:, 0]], axis=1)
    return y @ w_out, (state,)


def _s5_discretise(lam_re, lam_im, log_dt, b_re, b_im):
    dt = jnp.exp(log_dt.astype(F32))[:, None]
    lr, li = lam_re.astype(F32), lam_im.astype(F32)
    mag = jnp.exp(lr * dt)
    ar, ai = mag * jnp.cos(li * dt), mag * jnp.sin(li * dt)
    den = lr * lr + li * li
    zr = ((ar - 1.0) * lr + ai * li) / den
    zi = (ai * lr - (ar - 1.0) * li) / den
    br_, bi_ = b_re.astype(F32), b_im.astype(F32)
    bbr = zr[..., None] * br_ - zi[..., None] * bi_
    bbi = zr[..., None] * bi_ + zi[..., None] * br_
    return ar, ai, bbr, bbi


def _s5_scan(ug, ar, ai, bbr, bbi, cr, ci, h0r, h0i, reverse):
    b_, l_ = ug.shape[:2]
    n = l_ // SCAN_CHUNK
    uc = ug.reshape(b_, n, SCAN_CHUNK, S5_G, S5_K).transpose(1, 0, 2, 3, 4)
    edge = 0 if reverse else -1

    def step(carry, ub):
        hr0, hi0 = carry
        xr = jnp.einsum('bcgk,gpk->bcgp', ub, bbr)
        xi = jnp.einsum('bcgk,gpk->bcgp', ub, bbi)
        hr, hi = _complex_linear_scan(jnp.broadcast_to(ar, xr.shape), jnp.broadcast_to(ai, xr.shape),
                                      xr, xi, hr0, hi0, reverse)
        y = jnp.einsum('gkp,bcgp->bcgk', cr, hr) - jnp.einsum('gkp,bcgp->bcgk', ci, hi)
        return (hr[:, edge], hi[:, edge]), y

    (hr, hi), ys = lax.scan(step, (h0r, h0i), uc, reverse=reverse)
    return ys.transpose(1, 0, 2, 3, 4).reshape(b_, l_, S5_G * S5_K), hr, hi


def _s5_mixer(h, h0r, h0i, w_in, lam_re, lam_im, log_dt, b_re, b_im, c_re, c_im, d_skip, w_glu):
    b_, l_, _ = h.shape
    u = (h @ w_in).astype(F32)
    ug = u.reshape(b_, l_, S5_G, S5_K)
    y = d_skip.astype(F32) * u
    fr, fi = [], []
    for d in range(2):
        ar, ai, bbr, bbi = _s5_discretise(lam_re[d], lam_im[d], log_dt[d], b_re[d], b_im[d])
        yd, hr, hi = _s5_scan(ug, ar, ai, bbr, bbi, c_re[d].astype(F32), c_im[d].astype(F32),
                              h0r[:, d].astype(F32), h0i[:, d].astype(F32), reverse=(d == 1))
        y = y + yd
        fr.append(hr)
        fi.append(hi)
    z = jax.nn.gelu(y).astype(h.dtype)
    val, gate = jnp.split(z @ w_glu, 2, axis=-1)
    return val * jax.nn.sigmoid(gate), (jnp.stack(fr, axis=1), jnp.stack(fi, axis=1))


def _rope_axis(x, pos):
    half = x.shape[-1] // 2
    freq = ROPE_BASE ** (-jnp.arange(half, dtype=F32) / half)
    ang = pos.astype(F32)[:, None] * freq[None, :]
    cos = jnp.cos(ang)[None, :, None, None, :]
    sin = jnp.sin(ang)[None, :, None, None, :]
    x1, x2 = x[..., :half].astype(F32), x[..., half:].astype(F32)
    return jnp.concatenate([x1 * cos - x2 * sin, x2 * cos + x1 * sin], axis=-1).astype(x.dtype)


def _axial_rope(x):
    l_ = x.shape[1]
    rows = l_ // GRID_W
    row = jnp.repeat(jnp.arange(rows), GRID_W)
    col = jnp.tile(jnp.arange(GRID_W), rows)
    half = DA_DH // 2
    return jnp.concatenate([_rope_axis(x[..., :half], row), _rope_axis(x[..., half:], col)], axis=-1)


def _diff_attend(q, k, v, lam):
    b_, lq = q.shape[:2]
    qb = q.reshape(b_, lq // Q_BLOCK, Q_BLOCK, DA_HEADS, 2, DA_DH).transpose(1, 0, 2, 3, 4, 5)
    vf = v.astype(F32)
    scale = DA_DH ** -0.5

    def one_block(qi):
        s = jnp.einsum('bqhmd,bkhmd->bhmqk', qi, k).astype(F32) * scale
        p = jax.nn.softmax(s, axis=-1)
        a = p[:, :, 0] - lam * p[:, :, 1]
        return jnp.einsum('bhqk,bkhe->bqhe', a, vf)

    o = lax.map(one_block, qb)
    return o.transpose(1, 0, 2, 3, 4).reshape(b_, lq, DA_HEADS, 2 * DA_DH)


def _dattn_mixer(h, layer_idx, w_qkv, lam_p, subln_g, w_out, k_ctx, v_ctx):
    b_, l_, _ = h.shape
    q, k, v = jnp.split(h @ w_qkv, 3, axis=-1)
    q = q.reshape(b_, l_, DA_HEADS, 2, DA_DH)
    k = k.reshape(b_, l_, DA_HEADS, 2, DA_DH)
    v = v.reshape(b_, l_, DA_HEADS, 2 * DA_DH)
    lam_init = 0.8 - 0.6 * math.exp(-0.3 * layer_idx)
    lp = lam_p.astype(F32)
    lam = jnp.exp(jnp.sum(lp[0] * lp[1])) - jnp.exp(jnp.sum(lp[2] * lp[3])) + lam_init
    if k_ctx is None:
        keys, vals, state = k, v, (k, v)
    else:
        q = _axial_rope(q)
        keys = jnp.concatenate([k_ctx.astype(h.dtype), _axial_rope(k)], axis=1)
        vals = jnp.concatenate([v_ctx.astype(h.dtype), v], axis=1)
        state = ()
    o = _diff_attend(q, keys, vals, lam)
    o = o * lax.rsqrt(jnp.mean(jnp.square(o), axis=-1, keepdims=True) + LN_EPS)
    o = o * subln_g.astype(F32) * (1.0 - lam_init)
    return o.astype(h.dtype).reshape(b_, l_, D_MODEL) @ w_out, state


def _peer(h, wq, sub_keys, w_down, w_up):
    b_, l_, d_ = h.shape
    blocks = h.reshape(-1, PEER_BLOCK, d_)
    nc = PEER_TOPK * PEER_TOPK

    def one_block(xb):
        q = (xb @ wq).reshape(PEER_BLOCK, PEER_HEADS, 2, PEER_DKH)
        s = jnp.einsum('thsk,hsnk->thsn', q, sub_keys).astype(F32)
        s1, i1 = lax.top_k(s[:, :, 0], PEER_TOPK)
        s2, i2 = lax.top_k(s[:, :, 1], PEER_TOPK)
        cand = (s1[..., :, None] + s2[..., None, :]).reshape(PEER_BLOCK, PEER_HEADS, nc)
        cidx = (i1[..., :, None] * PEER_NK + i2[..., None, :]).reshape(PEER_BLOCK, PEER_HEADS, nc)
        sc, pos = lax.top_k(cand, PEER_TOPK)
        idx = jnp.take_along_axis(cidx, pos, axis=-1)
        g = jax.nn.softmax(sc, axis=-1)
        u = w_down[idx]
        act = g * jax.nn.gelu(jnp.einsum('thkd,td->thk', u, xb).astype(F32))
        v = w_up[idx]
        return jnp.einsum('thk,thkd->td', act.astype(xb.dtype), v)

    return lax.map(one_block, blocks).reshape(b_, l_, d_)


def setup_inputs(seed: int = 0) -> dict:
    key = jax.random.key(seed)
    ks = iter(jax.random.split(key, 64))

    def nrm(shape, scale):
        return jax.random.normal(next(ks), shape, F32) * scale

    def unif(shape, lo, hi):
        return jax.random.uniform(next(ks), shape, F32, lo, hi)

    dq = D_MODEL ** -0.5
    x_prompt = nrm((BATCH, SEQ, D_MODEL), 1.0)
    x_sample = nrm((DEC_BATCH, DEC_SEQ, D_MODEL), 1.0)
    c = nrm((DEC_BATCH, D_MODEL), 1.0)
    state_rglru = nrm((DEC_BATCH, N_A, 2, D_RNN), 1.0)
    state_s5_re = nrm((DEC_BATCH, N_B, 2, S5_G, S5_P), 1.0)
    state_s5_im = nrm((DEC_BATCH, N_B, 2, S5_G, S5_P), 1.0)
    cache_dattn_k = nrm((DEC_BATCH, N_C, PAST_LEN, DA_HEADS, 2, DA_DH), 1.0)
    cache_dattn_v = nrm((DEC_BATCH, N_C, PAST_LEN, DA_HEADS, 2 * DA_DH), 1.0)
    c_ctx = nrm((D_MODEL,), 1.0)
    ada_w = nrm((DEPTH, D_MODEL, 6 * D_MODEL), 0.5 * dq)
    ada_b = nrm((DEPTH, 6 * D_MODEL), 0.01)
    ln_g = 1.0 + nrm((DEPTH, 2, D_MODEL), 0.02)
    ln_b = nrm((DEPTH, 2, D_MODEL), 0.02)
    rg_w_in = nrm((N_A, D_MODEL, 2 * D_RNN), dq)
    rg_conv_w = nrm((N_A, CONV_W, D_RNN), CONV_W ** -0.5)
    rg_conv_b = nrm((N_A, D_RNN), 0.01)
    rg_gate_w = nrm((N_A, 2, 2, RG_BLOCKS, RG_BS, RG_BS), RG_BS ** -0.5)
    rg_gate_b = nrm((N_A, 2, 2, D_RNN), 0.01)
    p = unif((N_A, 2, D_RNN), 0.9, 0.999) ** (1.0 / RG_C)
    rg_lambda = jnp.log(p) - jnp.log1p(-p)
    rg_w_out = nrm((N_A, D_RNN, D_MODEL), D_RNN ** -0.5 * BETA)
    s5_w_in = nrm((N_B, D_MODEL, D_S5), dq)
    s5_lam_re = -0.5 + nrm((N_B, 2, S5_G, S5_P), 0.01)
    s5_lam_im = math.pi * jnp.arange(S5_P, dtype=F32) + nrm((N_B, 2, S5_G, S5_P), 0.01)
    s5_log_dt = jnp.log(unif((N_B, 2, S5_G), 0.001, 0.1))
    s5_b_re = nrm((N_B, 2, S5_G, S5_P, S5_K), (2 * S5_K) ** -0.5)
    s5_b_im = nrm((N_B, 2, S5_G, S5_P, S5_K), (2 * S5_K) ** -0.5)
    s5_c_re = nrm((N_B, 2, S5_G, S5_K, S5_P), S5_P ** -0.5)
    s5_c_im = nrm((N_B, 2, S5_G, S5_K, S5_P), S5_P ** -0.5)
    s5_d = nrm((N_B, D_S5), 0.5)
    s5_w_glu = jnp.concatenate([nrm((N_B, D_S5, D_MODEL), D_S5 ** -0.5 * BETA),
                                nrm((N_B, D_S5, D_MODEL), D_S5 ** -0.5)], axis=-1)
    da_w_qkv = nrm((N_C, D_MODEL, 3 * D_MODEL), dq)
    da_lambda = nrm((N_C, 4, DA_DH), 0.1)
    da_subln_g = 1.0 + nrm((N_C, 2 * DA_DH), 0.02)
    da_w_out = nrm((N_C, D_MODEL, D_MODEL), dq * BETA)
    peer_wq = nrm((DEPTH, D_MODEL, PEER_HEADS * PEER_DK), dq)
    peer_keys = nrm((DEPTH, PEER_HEADS, 2, PEER_NK, PEER_DKH), PEER_DKH ** -0.5)
    peer_down = nrm((DEPTH, PEER_N, D_MODEL), dq)
    peer_up = nrm((DEPTH, PEER_N, D_MODEL), BETA * (PEER_HEADS * PEER_TOPK) ** -0.5)
    return {'x_prompt': x_prompt, 'x_sample': x_sample, 'c': c,
            'state_rglru': state_rglru, 'state_s5_re': state_s5_re, 'state_s5_im': state_s5_im,
            'cache_dattn_k': cache_dattn_k, 'cache_dattn_v': cache_dattn_v, 'c_ctx': c_ctx,
            'ada_w': ada_w, 'ada_b': ada_b, 'ln_g': ln_g, 'ln_b': ln_b,
            'rg_w_in': rg_w_in, 'rg_conv_w': rg_conv_w, 'rg_conv_b': rg_conv_b,
            'rg_gate_w': rg_gate_w, 'rg_gate_b': rg_gate_b, 'rg_lambda': rg_lambda, 'rg_w_out': rg_w_out,
            's5_w_in': s5_w_in, 's5_lam_re': s5_lam_re, 's5_lam_im': s5_lam_im, 's5_log_dt': s5_log_dt,
            's5_b_re': s5_b_re, 's5_b_im': s5_b_im, 's5_c_re': s5_c_re, 's5_c_im': s5_c_im,
            's5_d': s5_d, 's5_w_glu': s5_w_glu,
            'da_w_qkv': da_w_qkv, 'da_lambda': da_lambda, 'da_subln_g': da_subln_g, 'da_w_out': da_w_out,
            'peer_wq': peer_wq, 'peer_keys': peer_keys, 'peer_down': peer_down, 'peer_up': peer_up}


def reference(x_prompt, x_sample, c, state_rglru, state_s5_re, state_s5_im, cache_dattn_k, cache_dattn_v,
              c_ctx, ada_w, ada_b, ln_g, ln_b, rg_w_in, rg_conv_w, rg_conv_b, rg_gate_w, rg_gate_b,
              rg_lambda, rg_w_out, s5_w_in, s5_lam_re, s5_lam_im, s5_log_dt, s5_b_re, s5_b_im, s5_c_re,
              s5_c_im, s5_d, s5_w_glu, da_w_qkv, da_lambda, da_subln_g, da_w_out, peer_wq, peer_keys,
              peer_down, peer_up):

    def mixer(i, h, cache):
        kind, j = i % N_MIXERS, i // N_MIXERS
        if kind == 0:
            return _rglru_mixer(h, cache[0], rg_w_in[j], rg_conv_w[j], rg_conv_b[j], rg_gate_w[j],
                                rg_gate_b[j], rg_lambda[j], rg_w_out[j])
        if kind == 1:
            return _s5_mixer(h, cache[0], cache[1], s5_w_in[j], s5_lam_re[j], s5_lam_im[j], s5_log_dt[j],
                             s5_b_re[j], s5_b_im[j], s5_c_re[j], s5_c_im[j], s5_d[j], s5_w_glu[j])
        k_ctx, v_ctx = cache if cache is not None else (None, None)
        return _dattn_mixer(h, i, da_w_qkv[j], da_lambda[j], da_subln_g[j], da_w_out[j], k_ctx, v_ctx)

    def layer(i, x, cond, cache):
        mods = _modulation(cond, ada_w[i], ada_b[i])
        y, st = mixer(i, _modulate(x, mods[:, 0], mods[:, 1]), cache)
        x = _post_norm(x, mods[:, 2], y, ln_g[i, 0], ln_b[i, 0])
        z = _peer(_modulate(x, mods[:, 3], mods[:, 4]), peer_wq[i], peer_keys[i], peer_down[i], peer_up[i])
        x = _post_norm(x, mods[:, 5], z, ln_g[i, 1], ln_b[i, 1])
        return x, st

    xp = x_prompt
    bp = x_prompt.shape[0]
    cond_ctx = c_ctx[None, :]
    rg_new, s5r_new, s5i_new, k_new, v_new = [], [], [], [], []
    for i in range(DEPTH):
        kind = i % N_MIXERS
        if kind == 0:
            cache = (jnp.zeros((bp, 2, D_RNN), F32),)
        elif kind == 1:
            cache = (jnp.zeros((bp, 2, S5_G, S5_P), F32), jnp.zeros((bp, 2, S5_G, S5_P), F32))
        else:
            cache = None
        xp, st = layer(i, xp, cond_ctx, cache)
        if kind == 0:
            rg_new.append(st[0])
        elif kind == 1:
            s5r_new.append(st[0])
            s5i_new.append(st[1])
        else:
            k_new.append(st[0])
            v_new.append(st[1])
    y_prompt = xp

    xs = x_sample
    for i in range(DEPTH):
        kind, j = i % N_MIXERS, i // N_MIXERS
        if kind == 0:
            cache = (state_rglru[:, j],)
        elif kind == 1:
            cache = (state_s5_re[:, j], state_s5_im[:, j])
        else:
            cache = (cache_dattn_k[:, j], cache_dattn_v[:, j])
        xs, _ = layer(i, xs, c, cache)
    y_sample = xs

    new_state_rglru = jnp.stack(rg_new, axis=1)
    new_state_s5_re = jnp.stack(s5r_new, axis=1)
    new_state_s5_im = jnp.stack(s5i_new, axis=1)
    new_cache_dattn_k = jnp.stack(k_new, axis=1)
    new_cache_dattn_v = jnp.stack(v_new, axis=1)
    return (y_prompt, y_sample, new_state_rglru, new_state_s5_re, new_state_s5_im, new_cache_dattn_k, new_cache_dattn_v)
```

```python
import math
from contextlib import ExitStack
import numpy as np
import concourse.bass as bass
import concourse.mybir as mybir
from concourse.bass_utils import run_bass_kernel_spmd

F32 = mybir.dt.float32
I32 = mybir.dt.int32
U32 = mybir.dt.uint32
ALU = mybir.AluOpType
AF = mybir.ActivationFunctionType
AX = mybir.AxisListType

D = 2048
KC = D // 128
DEPTH = 4
ALPHA = (2 * DEPTH) ** 0.25
LN_EPS = 1e-5
GRID_W = 64
SEM_LIMIT = 30000


class Buf:
    def __init__(self, t, name=""):
        self.t = t
        self.name = name
        self.w = {}
        self.r = {}

    def __getitem__(self, k):
        return self.t[k]


class Sched:
    def __init__(self, nc, es):
        self.nc = nc
        self.es = es
        self.E = {"pe": nc.tensor, "act": nc.scalar, "dve": nc.vector, "pool": nc.gpsimd, "sp": nc.sync}
        self.sems = {}
        self.nsem = 0
        self.cur = {}
        self.seen = {e: {} for e in self.E}
        self.dq = {}
        self.dqi = {}
        self.ninst = 0
        for e in ("pe", "act", "dve", "pool"):
            self._new_engine_sem(e)

    def _alloc(self, name):
        self.nsem += 1
        key = f"{name}_{self.nsem}"
        self.sems[key] = self.es.enter_context(self.nc.semaphore(key))
        return key

    def _new_engine_sem(self, e):
        self.cur[e] = [self._alloc("c" + e), 0]

    def _waits(self, e, reads, writes, skip_same_pe=False, disjoint=False, skip_prefix=None):
        need = {}
        for b in reads:
            for k, v in b.w.items():
                if need.get(k, 0) < v:
                    need[k] = v
        for b in writes:
            if not disjoint:
                for k, v in b.w.items():
                    if need.get(k, 0) < v:
                        need[k] = v
            for k, v in b.r.items():
                if need.get(k, 0) < v:
                    need[k] = v
        eng = self.E[e]
        seen = self.seen[e]
        for k, v in need.items():
            if skip_same_pe and k.startswith("cpe"):
                continue
            if skip_prefix is not None and k.startswith(skip_prefix):
                continue
            if seen.get(k, 0) >= v:
                continue
            eng.wait_ge(self.sems[k], v)
            seen[k] = v

    def _record(self, ev, reads, writes):
        k, v = ev
        for b in reads:
            if b.r.get(k, 0) < v:
                b.r[k] = v
        for b in writes:
            if b.w.get(k, 0) < v:
                b.w[k] = v

    def op(self, e, fn, reads=(), writes=(), accum=False, relax_self=False):
        self._waits(e, reads, writes, skip_same_pe=(e == "pe"), skip_prefix=("c" + e + "_") if relax_self else None)
        inst = fn(self.E[e])
        cur = self.cur[e]
        cur[1] += 1
        inst.then_inc(self.sems[cur[0]], 1)
        ev = (cur[0], cur[1])
        if accum:
            pass
        self._record(ev, reads, writes)
        self.ninst += 1
        if cur[1] >= SEM_LIMIT:
            self._new_engine_sem(e)
        return ev

    def dma(self, e, fn, reads=(), writes=(), nq=8, disjoint=False):
        if e not in self.dq:
            self.dq[e] = [[self._alloc("d" + e), 0] for _ in range(nq)]
            self.dqi[e] = 0
        q = self.dq[e]
        i = self.dqi[e]
        self.dqi[e] = (i + 1) % len(q)
        slot = q[i]
        if slot[1] + 16 > SEM_LIMIT:
            slot[0] = self._alloc("d" + e)
            slot[1] = 0
        eng = self.E[e]
        seen = self.seen[e]
        if slot[1] > 0 and seen.get(slot[0], 0) < slot[1]:
            eng.wait_ge(self.sems[slot[0]], slot[1])
            seen[slot[0]] = slot[1]
        self._waits(e, reads, writes, disjoint=disjoint)
        inst = fn(eng)
        slot[1] += 16
        inst.then_inc(self.sems[slot[0]], 16)
        ev = (slot[0], slot[1])
        self._record(ev, reads, writes)
        self.ninst += 1
        return ev

    def finish(self, bufs):
        self._waits("sp", bufs, ())
        eng = self.E["sp"]
        for e, q in self.dq.items():
            for k, v in q:
                if v > 0 and self.seen["sp"].get(k, 0) < v:
                    eng.wait_ge(self.sems[k], v)
                    self.seen["sp"][k] = v


class Ctx:
    def __init__(self, nc, es):
        self.nc = nc
        self.es = es
        self.s = Sched(nc, es)
        self.n = 0

    def sb(self, shape, dt=F32, name=None):
        self.n += 1
        nm = f"{name or 'sb'}_{self.n}"
        return Buf(self.es.enter_context(self.nc.sbuf_tensor(nm, list(shape), dt)), nm)

    def ps(self, shape=(128, 512), dt=F32, name=None):
        self.n += 1
        nm = f"{name or 'ps'}_{self.n}"
        return Buf(self.es.enter_context(self.nc.psum_tensor(nm, list(shape), dt)), nm)

    def dram(self, shape, dt=F32, name=None):
        self.n += 1
        nm = f"{name or 'dr'}_{self.n}"
        return Buf(self.nc.dram_tensor(nm, list(shape), dt).ap(), nm)


class Ring:
    def __init__(self, bufs):
        self.bufs = bufs
        self.i = 0

    def next(self):
        b = self.bufs[self.i]
        self.i = (self.i + 1) % len(self.bufs)
        return b


class K:
    def __init__(self, nc, es):
        self.c = Ctx(nc, es)
        self.nc = nc
        self.s = self.c.s
        c = self.c
        self.pg = [c.sb([128, 4352], F32, f"pg{i}") for i in range(9)]
        self.small = c.sb([128, 1024], F32, "small")
        self.small2 = c.sb([128, 1024], F32, "small2")
        self.small3 = c.sb([128, 512], F32, "small3")
        self.small_u = c.sb([128, 512], U32, "small_u")
        self.small_i = c.sb([128, 256], I32, "small_i")
        self.ident = c.sb([128, 128], F32, "ident")
        self.ones_row = c.sb([1, 128], F32, "ones_row")
        self.psb = [c.ps([128, 512], F32, f"psb{i}") for i in range(6)]
        self.psi = 0
        self.psh = [c.ps([128, 512], F32, f"psh{i}") for i in range(2)]
        self.pshi = 0
        self._mk_ident()

    def ps(self):
        p = self.psb[self.psi]
        self.psi = (self.psi + 1) % len(self.psb)
        return p

    @staticmethod
    def sub(parent, c0, c1):
        b = Buf(parent.t[:, c0:c1], parent.name + f"[{c0}:{c1}]")
        b.w = dict(parent.w)
        b.r = dict(parent.r)
        return b

    @staticmethod
    def merge_back(parent, subs):
        for sb_ in subs:
            for k_, v_ in sb_.w.items():
                if parent.w.get(k_, 0) < v_:
                    parent.w[k_] = v_
            for k_, v_ in sb_.r.items():
                if parent.r.get(k_, 0) < v_:
                    parent.r[k_] = v_

    def ps_held(self):
        p = self.psh[self.pshi]
        self.pshi = (self.pshi + 1) % len(self.psh)
        return p

    def v(self, fn, reads, writes):
        return self.s.op("dve", fn, reads, writes)

    def a(self, fn, reads, writes):
        return self.s.op("act", fn, reads, writes)

    def g(self, fn, reads, writes):
        return self.s.op("pool", fn, reads, writes)

    def mm(self, out_b, out_ap, lhsT_b, lhsT_ap, rhs_b, rhs_ap, start, stop):
        return self.s.op("pe", lambda e: e.matmul(out_ap, lhsT_ap, rhs_ap, start=start, stop=stop),
                         [lhsT_b, rhs_b], [out_b])

    def tr(self, out_b, out_ap, in_b, in_ap):
        return self.s.op("pe", lambda e: e.transpose(out_ap, in_ap, self.ident[0:in_ap.shape[0], 0:in_ap.shape[0]]),
                         [in_b, self.ident], [out_b])

    def ld(self, out_b, out_ap, in_b, in_ap, q="sp"):
        return self.s.dma(q, lambda e: e.dma_start(out=out_ap, in_=in_ap), [in_b], [out_b])

    def st(self, out_b, out_ap, in_b, in_ap, q="act"):
        return self.s.dma(q, lambda e: e.dma_start(out=out_ap, in_=in_ap), [in_b], [out_b], disjoint=True)

    def _mk_ident(self):
        idt = self.ident
        self.g(lambda e: e.memset(idt[:], 0.0), [], [idt])
        self.g(lambda e: e.affine_select(out=idt[:], in_=idt[:], pattern=[[-1, 128]], compare_op=ALU.not_equal,
                                         fill=1.0, base=0, channel_multiplier=1), [idt], [idt])
        orow = self.ones_row
        self.g(lambda e: e.memset(orow[:], 1.0), [], [orow])

    @staticmethod
    def v3(buf, a, b, off=0):
        return buf.t[:, off:off + a * b].rearrange("p (a b) -> p a b", a=a)

    def bc_load(self, dst, dst_ap, src_b, src_row_ap):
        n = dst_ap.shape[-1]
        self.s.dma("sp", lambda e: e.dma_start(out=dst_ap, in_=src_row_ap.to_broadcast([128, n])), [src_b], [dst])

    def mods_phase(self, conds, ncond, ada_w, ada_b, mods, depth):
        cs, M0, M1, M2, W0, W1, W2, W3 = self.pg[0], self.pg[1], self.pg[2], self.pg[3], self.pg[4], self.pg[5], self.pg[6], self.pg[7]
        Bt = self.pg[8]
        condT = self.small
        self.ld(cs, cs.t[0:ncond, 0:D], conds, conds.t[:, :])
        self.a(lambda e: e.activation(out=cs.t[0:ncond, 0:D], in_=cs.t[0:ncond, 0:D], func=AF.Silu), [cs], [cs])
        for c in range(KC):
            p = self.ps()
            self.tr(p, p.t[:, 0:ncond], cs, cs.t[0:ncond, c * 128:(c + 1) * 128])
            self.v(lambda e, p=p, c=c: e.tensor_copy(out=condT.t[:, c * ncond:(c + 1) * ncond], in_=p.t[:, 0:ncond]), [p], [condT])
        Ms = [M0, M1, M2]
        wring = Ring([(W0, W1), (W2, W3)])
        for l in range(depth):
            for j in range(24):
                wa, wb = wring.next()
                wsrc = ada_w.t[l].rearrange("(c p) n -> p c n", p=128)
                self.ld(wa, self.v3(wa, 8, 512), ada_w, wsrc[:, 0:8, j * 512:(j + 1) * 512])
                self.ld(wb, self.v3(wb, 8, 512), ada_w, wsrc[:, 8:16, j * 512:(j + 1) * 512])
                p = self.ps()
                for c in range(KC):
                    wt = wa if c < 8 else wb
                    self.mm(p, p.t[0:ncond, :], condT, condT.t[:, c * ncond:(c + 1) * ncond],
                            wt, self.v3(wt, 8, 512)[:, c % 8, :], c == 0, c == KC - 1)
                self.s.dma("sp", lambda e, l=l, j=j: e.dma_start(
                    out=Bt.t[0:ncond, 0:512], in_=ada_b.t[l:l + 1, j * 512:(j + 1) * 512].to_broadcast([ncond, 512])),
                    [ada_b], [Bt])
                m = Ms[j // 8]
                col = (j % 8) * 512
                self.v(lambda e, p=p, m=m, col=col: e.tensor_tensor(out=m.t[0:ncond, col:col + 512], in0=p.t[0:ncond, :],
                                                                     in1=Bt.t[0:ncond, 0:512], op=ALU.add), [p, Bt], [m])
            self.v(lambda e: e.tensor_scalar_add(out=M0.t[0:ncond, 2048:4096], in0=M0.t[0:ncond, 2048:4096], scalar1=1.0), [M0], [M0])
            self.v(lambda e: e.tensor_scalar_add(out=M2.t[0:ncond, 0:2048], in0=M2.t[0:ncond, 0:2048], scalar1=1.0), [M2], [M2])
            for i, m in enumerate(Ms):
                self.s.dma("act", lambda e, i=i, m=m, l=l: e.dma_start(out=mods.t[l, :, i * 4096:(i + 1) * 4096], in_=m.t[0:ncond, 0:4096]),
                           [m], [mods], disjoint=True)

    def load_bc(self, page, off, src_b, row_ap):
        self.bc_load(page, page.t[:, off:off + D], src_b, row_ap)

    def resid_ln_mod(self, L, x_b, x_ap, y_fn, gate_row, g_row, b_row, xo_b, xo_ap, shift_row, scale_row,
                     ht_b=None, h_b=None, final_b=None, final_ap=None):
        BC0, BC1, BC2 = self.pg[0], self.pg[1], self.pg[2]
        X, Y, Hh, Tt = self.pg[3], self.pg[4], self.pg[5], self.pg[6]
        st_ = self.small
        if y_fn is not None:
            self.load_bc(BC0, 0, *gate_row)
            self.load_bc(BC0, D, *g_row)
            self.load_bc(BC1, 0, *b_row)
        if shift_row is not None:
            self.load_bc(BC1, D, *shift_row)
            self.load_bc(BC2, 0, *scale_row)
        for t in range(L // 128):
            half = (t % 2) * D
            xs = X.t[:, half:half + D]
            self.ld(X, xs, x_b, x_ap(t))
            if y_fn is not None:
                ys = Y.t[:, half:half + D]
                y_fn(t, Y, ys)
                self.v(lambda e: e.tensor_tensor(out=ys, in0=ys, in1=BC0.t[:, 0:D], op=ALU.mult), [Y, BC0], [Y])
                self.v(lambda e: e.scalar_tensor_tensor(out=xs, in0=xs, scalar=ALPHA, in1=ys, op0=ALU.mult, op1=ALU.add), [X, Y], [X])
                so = (t % 2) * 64
                for q in range(4):
                    self.v(lambda e, q=q: e.bn_stats(out=st_.t[:, so + q * 6:so + q * 6 + 6], in_=xs[:, q * 512:(q + 1) * 512]), [X], [st_])
                self.v(lambda e: e.bn_aggr(out=st_.t[:, so + 32:so + 34], in_=st_.t[:, so:so + 24]), [st_], [st_])
                self.v(lambda e: e.tensor_scalar_add(out=st_.t[:, so + 34:so + 35], in0=st_.t[:, so + 33:so + 34], scalar1=LN_EPS), [st_], [st_])
                self.a(lambda e: e.sqrt(out=st_.t[:, so + 34:so + 35], in_=st_.t[:, so + 34:so + 35]), [st_], [st_])
                self.v(lambda e: e.reciprocal(out=st_.t[:, so + 34:so + 35], in_=st_.t[:, so + 34:so + 35]), [st_], [st_])
                self.v(lambda e: e.tensor_scalar(out=xs, in0=xs, scalar1=st_.t[:, so + 32:so + 33], scalar2=st_.t[:, so + 34:so + 35],
                                                 op0=ALU.subtract, op1=ALU.mult), [X, st_], [X])
                self.v(lambda e: e.tensor_tensor(out=xs, in0=xs, in1=BC0.t[:, D:2 * D], op=ALU.mult), [X, BC0], [X])
                self.v(lambda e: e.tensor_tensor(out=xs, in0=xs, in1=BC1.t[:, 0:D], op=ALU.add), [X, BC1], [X])
                if xo_b is not None:
                    self.st(xo_b, xo_ap(t), X, xs)
                if final_b is not None:
                    self.st(final_b, final_ap(t), X, xs)
            if shift_row is None:
                continue
            hs = Hh.t[:, half:half + D]
            self.v(lambda e: e.tensor_tensor(out=hs, in0=xs, in1=BC2.t[:, 0:D], op=ALU.mult), [X, BC2], [Hh])
            self.v(lambda e: e.tensor_tensor(out=hs, in0=hs, in1=BC1.t[:, D:2 * D], op=ALU.add), [Hh, BC1], [Hh])
            if h_b is not None:
                self.st(h_b, h_b.t[t * 128:(t + 1) * 128, :], Hh, hs)
            ts_ = Tt.t[:, half:half + D]
            for c4 in range(4):
                p = self.ps()
                for k in range(4):
                    c = c4 * 4 + k
                    self.tr(p, p.t[:, k * 128:(k + 1) * 128], Hh, hs[:, c * 128:(c + 1) * 128])
                self.a(lambda e, p=p, c4=c4: e.copy(out=ts_[:, c4 * 512:(c4 + 1) * 512], in_=p.t[:, :]), [p], [Tt])
            self.st(ht_b, ht_b.t.rearrange("(c p) l -> p c l", p=128)[:, :, t * 128:(t + 1) * 128],
                    Tt, ts_.rearrange("p (c l) -> p c l", c=KC))

    def linear_T(self, L, xt_b, w_b, w_ap, N, yt_b, yt_row0=0, kd=D, evac=None):
        kc = kd // 128
        XA, XB = self.pg[0], self.pg[1]
        Wr = Ring([self.pg[2], self.pg[3]])
        Or = Ring([self.pg[4], self.pg[5]])
        TT = min(512, L)
        xsrc = xt_b.t.rearrange("(c p) l -> p c l", p=128)
        wsrc = w_ap.rearrange("(c p) n -> p c n", p=128)
        for t in range(L // TT):
            hk = (kc + 1) // 2
            self.ld(XA, self.v3(XA, hk, TT), xt_b, xsrc[:, 0:hk, t * TT:(t + 1) * TT])
            if kc > hk:
                self.ld(XB, self.v3(XB, kc - hk, TT), xt_b, xsrc[:, hk:kc, t * TT:(t + 1) * TT])
            for oc in range(N // 128):
                w = Wr.next()
                self.ld(w, self.v3(w, kc, 128), w_b, wsrc[:, :, oc * 128:(oc + 1) * 128])
                p = self.ps()
                for c in range(kc):
                    xb = XA if c < hk else XB
                    xv = self.v3(xb, hk if c < hk else kc - hk, TT)[:, c if c < hk else c - hk, :]
                    self.mm(p, p.t[:, 0:TT], w, self.v3(w, kc, 128)[:, c, :], xb, xv, c == 0, c == kc - 1)
                o = Or.next()
                oi = (oc % 8) * 512
                if evac is None:
                    self.a(lambda e, p=p, o=o, oi=oi: e.copy(out=o.t[:, oi:oi + TT], in_=p.t[:, 0:TT]), [p], [o])
                else:
                    evac(p, o, oi, oc, TT)
                r0 = yt_row0 + oc * 128
                self.st(yt_b, yt_b.t[r0:r0 + 128, t * TT:(t + 1) * TT], o, o.t[:, oi:oi + TT])

    def linear(self, L, xt_b, w_b, w_ap, N, y_b, y_ap, kd=D):
        kc = kd // 128
        WA, WB = self.pg[0], self.pg[1]
        Xr = Ring([self.pg[2], self.pg[3]])
        Or = Ring([self.pg[4], self.pg[5]])
        xsrc = xt_b.t.rearrange("(c p) l -> p c l", p=128)
        wsrc = w_ap.rearrange("(c p) n -> p c n", p=128)
        hk = (kc + 1) // 2
        for nb in range(N // 512):
            self.ld(WA, self.v3(WA, hk, 512), w_b, wsrc[:, 0:hk, nb * 512:(nb + 1) * 512])
            if kc > hk:
                self.ld(WB, self.v3(WB, kc - hk, 512), w_b, wsrc[:, hk:kc, nb * 512:(nb + 1) * 512])
            for t in range(L // 128):
                x = Xr.next()
                self.ld(x, self.v3(x, kc, 128), xt_b, xsrc[:, :, t * 128:(t + 1) * 128])
                p = self.ps()
                for c in range(kc):
                    wb_ = WA if c < hk else WB
                    wv = self.v3(wb_, hk if c < hk else kc - hk, 512)[:, c if c < hk else c - hk, :]
                    self.mm(p, p.t[:, :], x, self.v3(x, kc, 128)[:, c, :], wb_, wv, c == 0, c == kc - 1)
                o = Or.next()
                oi = (t % 8) * 512
                self.a(lambda e, p=p, o=o, oi=oi: e.copy(out=o.t[:, oi:oi + 512], in_=p.t[:, :]), [p], [o])
                self.st(y_b, y_ap(t, nb * 512, (nb + 1) * 512), o, o.t[:, oi:oi + 512])

    def gelu(self, xb, x_ap, tb, t_ap, ob, o_ap):
        self.v(lambda e: e.tensor_tensor(out=t_ap, in0=x_ap, in1=x_ap, op=ALU.mult), [xb], [tb])
        self.v(lambda e: e.tensor_scalar(out=t_ap, in0=t_ap, scalar1=0.044715, scalar2=1.0, op0=ALU.mult, op1=ALU.add), [tb], [tb])
        self.v(lambda e: e.tensor_tensor(out=t_ap, in0=t_ap, in1=x_ap, op=ALU.mult), [tb, xb], [tb])
        self.a(lambda e: e.activation(out=t_ap, in_=t_ap, func=AF.Sigmoid, scale=1.5957691216057308), [tb], [tb])
        self.v(lambda e: e.tensor_tensor(out=o_ap, in0=t_ap, in1=x_ap, op=ALU.mult), [tb, xb], [ob])

    def peer(self, L, ht_b, h_b, wq_b, wq_ap, keys_b, keys_ap, down_b, down_ap, up_b, up_ap, qt_b, z_b):
        self.linear_T(L, ht_b, wq_b, wq_ap, D, qt_b)
        KT, S, S2, HQ = self.pg[0], self.pg[1], self.pg[2], self.pg[3]
        Hh = self.sub(HQ, 0, 2048)
        Q = self.sub(HQ, 2176, 4224)
        Zs = [self.sub(self.pg[4], 0, 2048), self.sub(self.pg[4], 2176, 4224), self.sub(self.pg[5], 0, 2048), self.sub(self.pg[5], 2176, 4224)]
        Us = [self.sub(self.pg[6 + i // 2], (i % 2) * 2176, (i % 2) * 2176 + 2048) for i in range(6)]
        Ur = Ring(Us)
        Z = Zs[0]
        sm, sm2, smu, smi = self.small, self.small2, self.small_u, self.small_i
        ktv = self.v3(KT, 16, 128)
        tmpk = self.v3(KT, 16, 128, off=2048)
        self.ld(KT, tmpk, keys_b, keys_ap.rearrange("h s n k -> n (h s) k"))
        for c in range(16):
            p = self.ps()
            self.tr(p, p.t[:, 0:128], KT, tmpk[:, c, :])
            self.a(lambda e: e.copy(out=ktv[:, c, :], in_=p.t[:, 0:128]), [p], [KT])
        self.g(lambda e: e.iota(out=smi.t[:, 0:256], pattern=[[1, 256]], base=0, channel_multiplier=0), [], [smi])
        self.v(lambda e: e.tensor_copy(out=sm2.t[:, 768:1024], in_=smi.t[:, 0:256]), [smi], [sm2])
        iota_bc = sm2.t[:, 768:1024].unsqueeze(1).to_broadcast([128, 16, 256])
        qsrc = qt_b.t.rearrange("(c p) l -> p c l", p=128)
        Vv = sm.t[:, 0:256].rearrange("p (c k) -> p c k", c=16)
        If = sm.t[:, 256:512].rearrange("p (c k) -> p c k", c=16)
        SC = sm.t[:, 512:640].rearrange("p (h k) -> p h k", h=8)
        G = sm.t[:, 640:768].rearrange("p (h k) -> p h k", h=8)
        PRE = sm.t[:, 768:896]
        IDXf = sm.t[:, 896:1024].rearrange("p (h k) -> p h k", h=8)
        Iu = smu.t[:, 0:256].rearrange("p (c k) -> p c k", c=16)
        POSu = smu.t[:, 256:384].rearrange("p (h k) -> p h k", h=8)
        cand = sm2.t[:, 0:256]
        cidx = sm2.t[:, 256:512]
        POSf = sm2.t[:, 512:640].rearrange("p (h k) -> p h k", h=8)
        misc = sm2.t[:, 640:768]
        for t in range(L // 128):
            qv = self.v3(Q, 16, 128)
            self.ld(Q, qv, qt_b, qsrc[:, :, t * 128:(t + 1) * 128])
            self.ld(Hh, Hh.t[:, 0:D], h_b, h_b.t[t * 128:(t + 1) * 128, :])
            for b4 in range(4):
                p = self.ps()
                for k in range(4):
                    c = b4 * 4 + k
                    self.mm(p, p.t[:, k * 128:(k + 1) * 128], Q, qv[:, c, :], KT, ktv[:, c, :], True, True)
                self.a(lambda e: e.copy(out=S.t[:, b4 * 512:(b4 + 1) * 512], in_=p.t[:, :]), [p], [S])
            for c in range(16):
                sl = S.t[:, c * 128:(c + 1) * 128]
                sl2 = S2.t[:, c * 128:(c + 1) * 128]
                self.v(lambda e: e.max(out=Vv[:, c, 0:8], in_=sl), [S], [sm])
                self.v(lambda e: e.max_index(out=Iu[:, c, 0:8], in_max=Vv[:, c, 0:8], in_values=sl), [S, sm], [smu])
                self.v(lambda e: e.match_replace(out=sl2, in_to_replace=Vv[:, c, 0:8], in_values=sl, imm_value=-1e30), [S, sm], [S2])
                self.v(lambda e: e.max(out=Vv[:, c, 8:16], in_=sl2), [S2], [sm])
                self.v(lambda e: e.max_index(out=Iu[:, c, 8:16], in_max=Vv[:, c, 8:16], in_values=sl2), [S2, sm], [smu])
            self.v(lambda e: e.tensor_copy(out=sm.t[:, 256:512], in_=smu.t[:, 0:256]), [smu], [sm])
            for h in range(8):
                c3 = cand.rearrange("p (i j) -> p i j", i=16)
                x3 = cidx.rearrange("p (i j) -> p i j", i=16)
                v1 = Vv[:, 2 * h, :].unsqueeze(2).to_broadcast([128, 16, 16])
                v2 = Vv[:, 2 * h + 1, :].unsqueeze(1).to_broadcast([128, 16, 16])
                i1 = If[:, 2 * h, :].unsqueeze(2).to_broadcast([128, 16, 16])
                i2 = If[:, 2 * h + 1, :].unsqueeze(1).to_broadcast([128, 16, 16])
                self.v(lambda e: e.tensor_tensor(out=c3, in0=v1, in1=v2, op=ALU.add), [sm], [sm2])
                self.v(lambda e: e.tensor_scalar(out=x3, in0=i1, scalar1=128.0, scalar2=None, op0=ALU.mult), [sm], [sm2])
                self.v(lambda e: e.tensor_tensor(out=x3, in0=x3, in1=i2, op=ALU.add), [sm, sm2], [sm2])
                self.v(lambda e: e.max(out=SC[:, h, 0:8], in_=cand), [sm2], [sm])
                self.v(lambda e: e.max_index(out=POSu[:, h, 0:8], in_max=SC[:, h, 0:8], in_values=cand), [sm2, sm], [smu])
                cand2 = self.small3.t[:, 0:256]
                self.v(lambda e: e.match_replace(out=cand2, in_to_replace=SC[:, h, 0:8], in_values=cand, imm_value=-1e30), [sm2, sm], [self.small3])
                self.v(lambda e: e.max(out=SC[:, h, 8:16], in_=cand2), [self.small3], [sm])
                self.v(lambda e: e.max_index(out=POSu[:, h, 8:16], in_max=SC[:, h, 8:16], in_values=cand2), [self.small3, sm], [smu])
                self.v(lambda e: e.tensor_copy(out=POSf[:, h, :], in_=POSu[:, h, :]), [smu], [sm2])
                eq = S2.t[:, 0:4096].rearrange("p (k c) -> p k c", k=16)
                self.v(lambda e: e.tensor_tensor(out=eq, in0=iota_bc, in1=POSf[:, h, :].unsqueeze(2).to_broadcast([128, 16, 256]),
                                                 op=ALU.is_equal), [sm2], [S2])
                self.v(lambda e: e.tensor_tensor(out=eq, in0=eq, in1=cidx.unsqueeze(1).to_broadcast([128, 16, 256]), op=ALU.mult), [S2, sm2], [S2])
                self.v(lambda e: e.tensor_reduce(out=IDXf[:, h, :], in_=eq, axis=AX.X, op=ALU.add), [S2], [sm])
            self.v(lambda e: e.tensor_tensor(out=G, in0=SC, in1=SC[:, :, 0:1].to_broadcast([128, 8, 16]), op=ALU.subtract), [sm], [sm])
            self.a(lambda e: e.activation(out=sm.t[:, 640:768], in_=sm.t[:, 640:768], func=AF.Exp), [sm], [sm])
            self.v(lambda e: e.tensor_reduce(out=misc[:, 0:8], in_=G, axis=AX.X, op=ALU.add), [sm], [sm2])
            self.v(lambda e: e.reciprocal(out=misc[:, 0:8], in_=misc[:, 0:8]), [sm2], [sm2])
            self.v(lambda e: e.tensor_tensor(out=G, in0=G, in1=misc[:, 0:8].unsqueeze(2).to_broadcast([128, 8, 16]), op=ALU.mult), [sm, sm2], [sm])
            self.v(lambda e: e.tensor_copy(out=smi.t[:, 0:128], in_=sm.t[:, 896:1024]), [sm], [smi])
            for sidx in range(128):
                u = Ur.next()
                self.s.dma("pool", lambda e: e.indirect_dma_start(
                    out=u.t[:, 0:D], out_offset=None, in_=down_ap,
                    in_offset=bass.IndirectOffsetOnAxis(ap=smi.t[:, sidx:sidx + 1], axis=0)), [down_b, smi], [u], nq=8)
                self.s.op("dve", lambda e: e.scalar_tensor_tensor(out=u.t[:, 0:D], in0=u.t[:, 0:D], scalar=1.0, in1=Hh.t[:, 0:D],
                                                                  op0=ALU.mult, op1=ALU.mult, accum_out=PRE[:, sidx:sidx + 1]),
                          [u, Hh], [u, sm], relax_self=(sidx > 0))
            self.gelu(sm, PRE, sm2, misc, sm2, misc)
            self.v(lambda e: e.tensor_tensor(out=misc, in0=misc, in1=sm.t[:, 640:768], op=ALU.mult), [sm2, sm], [sm2])
            for sidx in range(128):
                u = Ur.next()
                za = Zs[sidx % 4]
                self.s.dma("pool", lambda e: e.indirect_dma_start(
                    out=u.t[:, 0:D], out_offset=None, in_=up_ap,
                    in_offset=bass.IndirectOffsetOnAxis(ap=smi.t[:, sidx:sidx + 1], axis=0)), [up_b, smi], [u], nq=8)
                if sidx < 4:
                    self.v(lambda e: e.tensor_scalar(out=za.t[:, 0:D], in0=u.t[:, 0:D], scalar1=misc[:, sidx:sidx + 1], scalar2=None, op0=ALU.mult),
                           [u, sm2], [za])
                else:
                    self.v(lambda e: e.scalar_tensor_tensor(out=za.t[:, 0:D], in0=u.t[:, 0:D], scalar=misc[:, sidx:sidx + 1], in1=za.t[:, 0:D],
                                                            op0=ALU.mult, op1=ALU.add), [u, sm2, za], [za])
            self.v(lambda e: e.tensor_tensor(out=Zs[0].t[:, 0:D], in0=Zs[0].t[:, 0:D], in1=Zs[1].t[:, 0:D], op=ALU.add), [Zs[0], Zs[1]], [Zs[0]])
            self.v(lambda e: e.tensor_tensor(out=Zs[2].t[:, 0:D], in0=Zs[2].t[:, 0:D], in1=Zs[3].t[:, 0:D], op=ALU.add), [Zs[2], Zs[3]], [Zs[2]])
            self.v(lambda e: e.tensor_tensor(out=Zs[0].t[:, 0:D], in0=Zs[0].t[:, 0:D], in1=Zs[2].t[:, 0:D], op=ALU.add), [Zs[0], Zs[2]], [Zs[0]])
            self.st(z_b, z_b.t[t * 128:(t + 1) * 128, :], Z, Z.t[:, 0:D])
        self.merge_back(HQ, [Hh, Q])
        self.merge_back(self.pg[4], Zs[0:2])
        self.merge_back(self.pg[5], Zs[2:4])
        for i in range(6):
            self.merge_back(self.pg[6 + i // 2], [Us[i]])

    def load_cols(self, dst_b, dst_ap, src_b, src_rows_ap, stage_b, stage_ap):
        n = src_rows_ap.shape[0]
        self.ld(stage_b, stage_ap[0:n, 0:128], src_b, src_rows_ap)
        p = self.ps()
        self.tr(p, p.t[:, 0:n], stage_b, stage_ap[0:n, 0:128])
        self.v(lambda e: e.tensor_copy(out=dst_ap, in_=p.t[:, 0:n]), [p], [dst_b])

    def rglru(self, L, ht_b, w_in, conv_w, conv_b, gate_w, gate_b, lam, w_out, h0, state_out, gxt_b, yt_b, ymix_b):
        self.linear_T(L, ht_b, w_in[0], w_in[1], 4096, gxt_b)
        XC = [self.pg[0], self.pg[1]]
        A, BB, HS0, HS1, GBp, GW, TT_ = self.pg[2], self.pg[3], self.pg[4], self.pg[5], self.pg[6], self.pg[7], self.pg[8]
        P = self.small3
        sm, sm2 = self.small, self.small2
        stage = TT_
        Pt = P.t
        self.load_cols(P, Pt[:, 0:64], conv_w[0], conv_w[1].rearrange("j (c p) -> (j c) p", p=128), stage, stage.t)
        self.load_cols(P, Pt[:, 64:80], conv_b[0], conv_b[1].rearrange("(c p) -> c p", p=128), stage, stage.t)
        self.load_cols(P, Pt[:, 80:144], gate_b[0], gate_b[1].rearrange("d g (c p) -> (d g c) p", p=128), stage, stage.t)
        self.load_cols(P, Pt[:, 144:176], lam[0], lam[1].rearrange("d (c p) -> (d c) p", p=128), stage, stage.t)
        if h0 is not None:
            self.load_cols(P, Pt[:, 176:208], h0[0], h0[1].rearrange("d (c p) -> (d c) p", p=128), stage, stage.t)
        else:
            self.v(lambda e: e.memset(Pt[:, 176:208], 0.0), [], [P])
        self.a(lambda e: e.activation(out=Pt[:, 208:240], in_=Pt[:, 144:176], func=AF.Exp, scale=-1.0), [P], [P])
        self.a(lambda e: e.activation(out=Pt[:, 208:240], in_=Pt[:, 208:240], func=AF.Ln, bias=1.0), [P], [P])
        self.v(lambda e: e.tensor_scalar(out=Pt[:, 208:240], in0=Pt[:, 208:240], scalar1=-8.0, scalar2=None, op0=ALU.mult), [P], [P])
        NT = max(1, L // 512)
        TW = min(512, L)
        for n in range(8):
            for oc in range(2):
                cc = 2 * n + oc
                xr = A.t[:, 0:L]
                xc = XC[oc].t[:, 0:L]
                self.ld(A, xr, gxt_b, gxt_b.t[2048 + cc * 128:2048 + (cc + 1) * 128, :])
                self.v(lambda e: e.tensor_scalar(out=xc, in0=xr, scalar1=Pt[:, 32 + cc:33 + cc], scalar2=Pt[:, 64 + cc:65 + cc],
                                                 op0=ALU.mult, op1=ALU.add), [A, P], [XC[oc]])
                self.v(lambda e: e.scalar_tensor_tensor(out=xc[:, 2:L], in0=xr[:, 0:L - 2], scalar=Pt[:, cc:cc + 1], in1=xc[:, 2:L],
                                                        op0=ALU.mult, op1=ALU.add), [A, P, XC[oc]], [XC[oc]])
                self.v(lambda e: e.scalar_tensor_tensor(out=xc[:, 1:L], in0=xr[:, 0:L - 1], scalar=Pt[:, 16 + cc:17 + cc], in1=xc[:, 1:L],
                                                        op0=ALU.mult, op1=ALU.add), [A, P, XC[oc]], [XC[oc]])
                self.v(lambda e: e.scalar_tensor_tensor(out=xc[:, 0:L - 1], in0=xr[:, 1:L], scalar=Pt[:, 48 + cc:49 + cc], in1=xc[:, 0:L - 1],
                                                        op0=ALU.mult, op1=ALU.add), [A, P, XC[oc]], [XC[oc]])
            gwv = GW.t[:, 0:2048].rearrange("p (q kc j) -> p q kc j", q=4, kc=2)
            for d in range(2):
                for g_ in range(2):
                    self.ld(GW, gwv[:, d * 2 + g_], gate_w[0], gate_w[1][d, g_, n].rearrange("(kc p) j -> p kc j", p=128))
            for oc in range(2):
                cc = 2 * n + oc
                for d in range(2):
                    HS = HS0 if d == 0 else HS1
                    for tt in range(NT):
                        ts_ = slice(tt * TW, (tt + 1) * TW)
                        rt = sm.t[:, 0:TW]
                        it = sm.t[:, 512:512 + TW]
                        tmp = sm2.t[:, 0:TW]
                        for g_, dst in ((0, rt), (1, it)):
                            p = self.ps()
                            for kc in range(2):
                                self.mm(p, p.t[:, 0:TW], GW, gwv[:, d * 2 + g_, kc, oc * 128:(oc + 1) * 128],
                                        XC[kc], XC[kc].t[:, ts_], kc == 0, kc == 1)
                            bcol = 80 + (d * 2 + g_) * 16 + cc
                            self.a(lambda e: e.activation(out=dst, in_=p.t[:, 0:TW], func=AF.Sigmoid, bias=Pt[:, bcol:bcol + 1], scale=1.0),
                                   [p, P], [sm])
                        ncol = 208 + d * 16 + cc
                        self.a(lambda e: e.activation(out=A.t[:, ts_], in_=rt, func=AF.Exp, scale=Pt[:, ncol:ncol + 1]), [sm, P], [A])
                        self.v(lambda e: e.tensor_tensor(out=tmp, in0=A.t[:, ts_], in1=A.t[:, ts_], op=ALU.mult), [A], [sm2])
                        self.a(lambda e: e.activation(out=tmp, in_=tmp, func=AF.Sqrt, bias=1.0, scale=-1.0), [sm2], [sm2])
                        self.v(lambda e: e.tensor_tensor(out=tmp, in0=tmp, in1=it, op=ALU.mult), [sm2, sm], [sm2])
                        self.v(lambda e: e.tensor_tensor(out=BB.t[:, ts_], in0=tmp, in1=XC[oc].t[:, ts_], op=ALU.mult), [sm2, XC[oc]], [BB])
                    hcol = 176 + d * 16 + cc
                    if d == 0:
                        self.v(lambda e: e.tensor_tensor_scan(out=HS.t[:, 0:L], data0=A.t[:, 0:L], data1=BB.t[:, 0:L],
                                                              initial=Pt[:, hcol:hcol + 1], op0=ALU.mult, op1=ALU.add), [A, BB, P], [HS])
                        self.v(lambda e: e.tensor_copy(out=Pt[:, 240 + cc:241 + cc], in_=HS.t[:, L - 1:L]), [HS], [P])
                    else:
                        self.v(lambda e: e.tensor_tensor_scan(out=HS.t[:, 0:L][:, ::-1], data0=A.t[:, 0:L][:, ::-1], data1=BB.t[:, 0:L][:, ::-1],
                                                              initial=Pt[:, hcol:hcol + 1], op0=ALU.mult, op1=ALU.add), [A, BB, P], [HS])
                        self.v(lambda e: e.tensor_copy(out=Pt[:, 256 + cc:257 + cc], in_=HS.t[:, 0:1]), [HS], [P])
                gb = GBp.t[:, 0:L]
                self.ld(GBp, gb, gxt_b, gxt_b.t[cc * 128:(cc + 1) * 128, :])
                self.v(lambda e: e.tensor_tensor(out=HS0.t[:, 0:L], in0=HS0.t[:, 0:L], in1=HS1.t[:, 0:L], op=ALU.add), [HS0, HS1], [HS0])
                self.gelu(GBp, gb, TT_, TT_.t[:, 0:L], TT_, TT_.t[:, 0:L])
                self.v(lambda e: e.tensor_tensor(out=HS0.t[:, 0:L], in0=HS0.t[:, 0:L], in1=TT_.t[:, 0:L], op=ALU.mult), [HS0, TT_], [HS0])
                self.st(yt_b, yt_b.t[cc * 128:(cc + 1) * 128, :], HS0, HS0.t[:, 0:L])
        if state_out is not None:
            p = self.ps()
            self.tr(p, p.t[0:32, 0:128], P, Pt[:, 240:272])
            self.v(lambda e: e.tensor_copy(out=sm.t[0:32, 0:128], in_=p.t[0:32, 0:128]), [p], [sm])
            self.st(state_out[0], state_out[1].rearrange("d (c p) -> (d c) p", p=128), sm, sm.t[0:32, 0:128])
        self.linear(L, yt_b, w_out[0], w_out[1], D, ymix_b, lambda t, n0, n1: ymix_b.t[t * 128:(t + 1) * 128, n0:n1])

    def sincos(self, yb, y_ap, sb_, sin_ap, cb_, cos_ap, ib, i_ap, tb, t_ap):
        for off, (db, dst) in ((0.0, (sb_, sin_ap)), (0.25, (cb_, cos_ap))):
            if dst is None:
                continue
            self.v(lambda e: e.tensor_scalar_add(out=t_ap, in0=y_ap, scalar1=off), [yb], [tb])
            self.v(lambda e: e.tensor_copy(out=i_ap, in_=t_ap), [tb], [ib])
            self.v(lambda e: e.tensor_copy(out=dst, in_=i_ap), [ib], [db])
            self.v(lambda e: e.tensor_tensor(out=t_ap, in0=t_ap, in1=dst, op=ALU.subtract), [tb, db], [tb])
            self.v(lambda e: e.scalar_tensor_tensor(out=dst, in0=t_ap, scalar=0.5, in1=t_ap, op0=ALU.is_gt, op1=ALU.subtract), [tb], [db])
            self.v(lambda e: e.scalar_tensor_tensor(out=t_ap, in0=t_ap, scalar=-0.5, in1=dst, op0=ALU.is_lt, op1=ALU.subtract), [tb, db], [tb])
            self.a(lambda e: e.activation(out=dst, in_=t_ap, func=AF.Sin, scale=6.283185), [tb], [db])

    def s5(self, L, ht_b, w_in, lam_re, lam_im, log_dt, b_re, b_im, c_re, c_im, d_skip, w_glu, h0r, h0i, sor, soi,
           ut_b, zt_b, vg_b):
        self.linear_T(L, ht_b, w_in[0], w_in[1], D, ut_b)
        PP, PB, PT, PL, PU, PY, W0, W1, STG = self.pg
        sm, sm2, smi = self.small, self.small2, self.small_i
        TW = min(512, L)
        NT = L // TW
        inv2pi = 1.0 / (2.0 * math.pi)
        SL = {n: i * 64 for i, n in enumerate(["lr", "li", "dt", "mag", "th", "sn", "cs", "ar1", "ai", "den", "zr", "zi", "h0r", "h0i", "t0", "t1"])}

        def pp(d, name, j0=0, j1=64):
            o = d * 1024 + SL[name]
            return PP.t[:, o + j0:o + j1]
        STo = 2048
        DSo = 2304
        IOo = 2560
        self.g(lambda e: e.iota(out=smi.t[:, 0:256], pattern=[[1, 256]], base=1, channel_multiplier=0), [], [smi])
        self.v(lambda e: e.tensor_copy(out=PP.t[:, IOo:IOo + 256], in_=smi.t[:, 0:256]), [smi], [PP])
        self.v(lambda e: e.tensor_scalar_add(out=PP.t[:, IOo + 256:IOo + 512], in0=PP.t[:, IOo:IOo + 256], scalar1=256.0), [PP], [PP])
        self.load_cols(PP, PP.t[:, DSo:DSo + 16], d_skip[0], d_skip[1].rearrange("(c p) -> c p", p=128), STG, STG.t)
        pairv = lambda ap: ap.rearrange("(j g2) p -> j (g2 p)", g2=2)
        for d in range(2):
            self.load_cols(PP, pp(d, "lr"), lam_re[0], pairv(lam_re[1][d]), STG, STG.t)
            self.load_cols(PP, pp(d, "li"), lam_im[0], pairv(lam_im[1][d]), STG, STG.t)
            if h0r is not None:
                self.load_cols(PP, pp(d, "h0r"), h0r[0], pairv(h0r[1][d]), STG, STG.t)
                self.load_cols(PP, pp(d, "h0i"), h0i[0], pairv(h0i[1][d]), STG, STG.t)
            else:
                self.v(lambda e: e.memset(pp(d, "h0r"), 0.0), [], [PP])
                self.v(lambda e: e.memset(pp(d, "h0i"), 0.0), [], [PP])
            self.ld(STG, STG.t[0:1, 0:128], log_dt[0], log_dt[1][d:d + 1, :])
            p = self.ps()
            self.mm(p, p.t[:, 0:128], self.ones_row, self.ones_row.t[0:1, :], STG, STG.t[0:1, 0:128], True, True)
            pv = p.t[:, 0:128].rearrange("q (j g2) -> q j g2", g2=2)
            self.a(lambda e: e.activation(out=pp(d, "dt")[0:64, :], in_=pv[0:64, :, 0], func=AF.Exp), [p], [PP])
            self.a(lambda e: e.activation(out=pp(d, "dt")[64:128, :], in_=pv[64:128, :, 1], func=AF.Exp), [p], [PP])
            self.v(lambda e: e.tensor_tensor(out=pp(d, "mag"), in0=pp(d, "lr"), in1=pp(d, "dt"), op=ALU.mult), [PP], [PP])
            self.a(lambda e: e.activation(out=pp(d, "mag"), in_=pp(d, "mag"), func=AF.Exp), [PP], [PP])
            self.v(lambda e: e.tensor_tensor(out=pp(d, "th"), in0=pp(d, "li"), in1=pp(d, "dt"), op=ALU.mult), [PP], [PP])
            self.v(lambda e: e.tensor_scalar(out=pp(d, "th"), in0=pp(d, "th"), scalar1=inv2pi, scalar2=None, op0=ALU.mult), [PP], [PP])
            self.v(lambda e: e.tensor_copy(out=pp(d, "t1"), in_=pp(d, "th")), [PP], [PP])
            self.sincos(PP, pp(d, "t1"), PP, pp(d, "sn"), PP, pp(d, "cs"), smi, smi.t[:, 0:64], PP, pp(d, "t0"))
            self.v(lambda e: e.tensor_tensor(out=pp(d, "ar1"), in0=pp(d, "mag"), in1=pp(d, "cs"), op=ALU.mult), [PP], [PP])
            self.v(lambda e: e.tensor_scalar_add(out=pp(d, "ar1"), in0=pp(d, "ar1"), scalar1=-1.0), [PP], [PP])
            self.v(lambda e: e.tensor_tensor(out=pp(d, "ai"), in0=pp(d, "mag"), in1=pp(d, "sn"), op=ALU.mult), [PP], [PP])
            self.v(lambda e: e.tensor_tensor(out=pp(d, "den"), in0=pp(d, "lr"), in1=pp(d, "lr"), op=ALU.mult), [PP], [PP])
            self.v(lambda e: e.tensor_tensor(out=pp(d, "t0"), in0=pp(d, "li"), in1=pp(d, "li"), op=ALU.mult), [PP], [PP])
            self.v(lambda e: e.tensor_tensor(out=pp(d, "den"), in0=pp(d, "den"), in1=pp(d, "t0"), op=ALU.add), [PP], [PP])
            self.v(lambda e: e.reciprocal(out=pp(d, "den"), in_=pp(d, "den")), [PP], [PP])
            self.v(lambda e: e.tensor_tensor(out=pp(d, "zr"), in0=pp(d, "ar1"), in1=pp(d, "lr"), op=ALU.mult), [PP], [PP])
            self.v(lambda e: e.tensor_tensor(out=pp(d, "t0"), in0=pp(d, "ai"), in1=pp(d, "li"), op=ALU.mult), [PP], [PP])
            self.v(lambda e: e.tensor_tensor(out=pp(d, "zr"), in0=pp(d, "zr"), in1=pp(d, "t0"), op=ALU.add), [PP], [PP])
            self.v(lambda e: e.tensor_tensor(out=pp(d, "zr"), in0=pp(d, "zr"), in1=pp(d, "den"), op=ALU.mult), [PP], [PP])
            self.v(lambda e: e.tensor_tensor(out=pp(d, "zi"), in0=pp(d, "ai"), in1=pp(d, "lr"), op=ALU.mult), [PP], [PP])
            self.v(lambda e: e.tensor_tensor(out=pp(d, "t0"), in0=pp(d, "ar1"), in1=pp(d, "li"), op=ALU.mult), [PP], [PP])
            self.v(lambda e: e.tensor_tensor(out=pp(d, "zi"), in0=pp(d, "zi"), in1=pp(d, "t0"), op=ALU.subtract), [PP], [PP])
            self.v(lambda e: e.tensor_tensor(out=pp(d, "zi"), in0=pp(d, "zi"), in1=pp(d, "den"), op=ALU.mult), [PP], [PP])
            BR = W0.t[:, 0:1024].rearrange("p (j k) -> p j k", k=16)
            BI = W1.t[:, 0:1024].rearrange("p (j k) -> p j k", k=16)
            T1 = W0.t[:, 1024:2048].rearrange("p (j k) -> p j k", k=16)
            T2 = W1.t[:, 1024:2048].rearrange("p (j k) -> p j k", k=16)
            self.ld(W0, BR, b_re[0], b_re[1][d].rearrange("(j g2) p k -> (g2 p) j k", g2=2))
            self.ld(W1, BI, b_im[0], b_im[1][d].rearrange("(j g2) p k -> (g2 p) j k", g2=2))
            zrb = pp(d, "zr").unsqueeze(2).to_broadcast([128, 64, 16])
            zib = pp(d, "zi").unsqueeze(2).to_broadcast([128, 64, 16])
            BBr = PB.t[:, (d * 2) * 1024:(d * 2 + 1) * 1024].rearrange("p (j k) -> p j k", k=16)
            BBi = PB.t[:, (d * 2 + 1) * 1024:(d * 2 + 2) * 1024].rearrange("p (j k) -> p j k", k=16)
            self.v(lambda e: e.tensor_tensor(out=T1, in0=BR, in1=zrb, op=ALU.mult), [W0, PP], [W0])
            self.v(lambda e: e.tensor_tensor(out=T2, in0=BI, in1=zib, op=ALU.mult), [W1, PP], [W1])
            self.v(lambda e: e.tensor_tensor(out=BBr, in0=T1, in1=T2, op=ALU.subtract), [W0, W1], [PB])
            self.v(lambda e: e.tensor_tensor(out=T1, in0=BI, in1=zrb, op=ALU.mult), [W1, PP], [W0])
            self.v(lambda e: e.tensor_tensor(out=T2, in0=BR, in1=zib, op=ALU.mult), [W0, PP], [W1])
            self.v(lambda e: e.tensor_tensor(out=BBi, in0=T1, in1=T2, op=ALU.add), [W0, W1], [PB])
        self.v(lambda e: e.memset(STG.t[:, 0:2048], 0.0), [], [STG])
        zst = lambda kind, jj: STG.t[:, (kind * 4 + jj) * 128:(kind * 4 + jj + 1) * 128]
        lmat = lambda kind, jj: PL.t[:, (kind * 4 + jj) * 128:(kind * 4 + jj + 1) * 128]
        usrc = ut_b.t
        for fc in range(16):
            U = PU.t[:, 0:L]
            Y = PY.t[:, 0:L]
            self.ld(PU, U, ut_b, usrc[fc * 128:(fc + 1) * 128, :])
            self.v(lambda e: e.tensor_scalar(out=Y, in0=U, scalar1=PP.t[:, DSo + fc:DSo + fc + 1], scalar2=None, op0=ALU.mult), [PU, PP], [PY])
            for d in range(2):
                BBr = PB.t[:, (d * 2) * 1024:(d * 2 + 1) * 1024].rearrange("p (j k) -> p j k", k=16)
                BBi = PB.t[:, (d * 2 + 1) * 1024:(d * 2 + 2) * 1024].rearrange("p (j k) -> p j k", k=16)
                for jj in range(4):
                    j = fc * 4 + jj
                    for kind, BBx in ((0, BBr), (1, BBi)):
                        z = zst(kind, jj)
                        for g2 in range(2):
                            co = (2 * jj + g2) * 16
                            self.v(lambda e: e.tensor_copy(out=z[g2 * 64:(g2 + 1) * 64, co:co + 16], in_=BBx[g2 * 64:(g2 + 1) * 64, j, :]), [PB], [STG])
                        p = self.ps()
                        self.tr(p, p.t[:, 0:128], STG, z)
                        self.a(lambda e: e.copy(out=lmat(kind, jj), in_=p.t[:, 0:128]), [p], [PL])
                    for kind, cx in ((2, c_re), (3, c_im)):
                        z = zst(kind, jj)
                        for g2 in range(2):
                            ro = (2 * jj + g2) * 16
                            g = fc * 8 + 2 * jj + g2
                            self.s.dma("sp", lambda e: e.dma_start(out=z[ro:ro + 16, g2 * 64:(g2 + 1) * 64], in_=cx[1][d, g]), [cx[0]], [STG])
                        p = self.ps()
                        self.tr(p, p.t[:, 0:128], STG, z)
                        if kind == 2:
                            self.a(lambda e: e.copy(out=lmat(kind, jj), in_=p.t[:, 0:128]), [p], [PL])
                        else:
                            self.a(lambda e: e.mul(out=lmat(kind, jj), in_=p.t[:, 0:128], mul=-1.0), [p], [PL])
                    Ct = PT.t[:, (jj * 2) * 512:(jj * 2) * 512 + TW]
                    St = PT.t[:, (jj * 2 + 1) * 512:(jj * 2 + 1) * 512 + TW]
                    yv = W0.t[:, 0:TW]
                    self.v(lambda e: e.tensor_scalar(out=yv, in0=PP.t[:, IOo:IOo + TW], scalar1=pp(d, "th", j, j + 1), scalar2=None, op0=ALU.mult), [PP], [W0])
                    self.sincos(W0, yv, PT, St, PT, Ct, self.small_u, self.small_u.t[:, 0:TW].bitcast(I32), W0, W0.t[:, 512:512 + TW])
                for jj in range(4):
                    j = fc * 4 + jj
                    self.v(lambda e: e.tensor_copy(out=sm.t[:, jj * 2:jj * 2 + 1], in_=pp(d, "h0r", j, j + 1)), [PP], [sm])
                    self.v(lambda e: e.tensor_copy(out=sm.t[:, jj * 2 + 1:jj * 2 + 2], in_=pp(d, "h0i", j, j + 1)), [PP], [sm])
                order = range(NT) if d == 0 else range(NT - 1, -1, -1)
                for tt in order:
                    ts_ = slice(tt * TW, (tt + 1) * TW)
                    py = self.ps_held()
                    for jj in range(4):
                        j = fc * 4 + jj
                        Ct = PT.t[:, (jj * 2) * 512:(jj * 2) * 512 + TW]
                        St = PT.t[:, (jj * 2 + 1) * 512:(jj * 2 + 1) * 512 + TW]
                        if d == 1:
                            Ct = Ct[:, ::-1]
                            St = St[:, ::-1]
                        pr = self.ps()
                        pi_ = self.ps()
                        self.mm(pr, pr.t[:, 0:TW], PL, lmat(0, jj), PU, U[:, ts_], True, True)
                        self.mm(pi_, pi_.t[:, 0:TW], PL, lmat(1, jj), PU, U[:, ts_], True, True)
                        wv = lambda b, i: b.t[:, i * 512:i * 512 + TW]
                        t1, t2, kir, kii = wv(W0, 2), wv(W0, 3), wv(W0, 4), wv(W0, 5)
                        kr, ki, hr, hi = wv(W1, 2), wv(W1, 3), wv(W1, 4), wv(W1, 5)
                        self.v(lambda e: e.tensor_tensor(out=t1, in0=pr.t[:, 0:TW], in1=Ct, op=ALU.mult), [pr, PT], [W0])
                        self.v(lambda e: e.tensor_tensor(out=t2, in0=pi_.t[:, 0:TW], in1=St, op=ALU.mult), [pi_, PT], [W0])
                        self.v(lambda e: e.tensor_tensor(out=kir, in0=t1, in1=t2, op=ALU.add), [W0], [W0])
                        self.v(lambda e: e.tensor_tensor(out=t1, in0=pi_.t[:, 0:TW], in1=Ct, op=ALU.mult), [pi_, PT], [W0])
                        self.v(lambda e: e.tensor_tensor(out=t2, in0=pr.t[:, 0:TW], in1=St, op=ALU.mult), [pr, PT], [W0])
                        self.v(lambda e: e.tensor_tensor(out=kii, in0=t1, in1=t2, op=ALU.subtract), [W0], [W0])
                        mdec = pp(d, "mag", j, j + 1).to_broadcast([128, TW])
                        rv = (lambda ap: ap[:, ::-1]) if d == 1 else (lambda ap: ap)
                        self.v(lambda e: e.tensor_tensor_scan(out=rv(kr), data0=mdec, data1=rv(kir), initial=sm.t[:, jj * 2:jj * 2 + 1],
                                                              op0=ALU.mult, op1=ALU.add), [W0, PP, sm], [W1])
                        self.v(lambda e: e.tensor_tensor_scan(out=rv(ki), data0=mdec, data1=rv(kii), initial=sm.t[:, jj * 2 + 1:jj * 2 + 2],
                                                              op0=ALU.mult, op1=ALU.add), [W0, PP, sm], [W1])
                        self.v(lambda e: e.tensor_tensor(out=t1, in0=kr, in1=Ct, op=ALU.mult), [W1, PT], [W0])
                        self.v(lambda e: e.tensor_tensor(out=t2, in0=ki, in1=St, op=ALU.mult), [W1, PT], [W0])
                        self.v(lambda e: e.tensor_tensor(out=hr, in0=t1, in1=t2, op=ALU.subtract), [W0], [W1])
                        self.v(lambda e: e.tensor_tensor(out=t1, in0=ki, in1=Ct, op=ALU.mult), [W1, PT], [W0])
                        self.v(lambda e: e.tensor_tensor(out=t2, in0=kr, in1=St, op=ALU.mult), [W1, PT], [W0])
                        self.v(lambda e: e.tensor_tensor(out=hi, in0=t1, in1=t2, op=ALU.add), [W0], [W1])
                        ecol = TW - 1 if d == 0 else 0
                        self.v(lambda e: e.tensor_copy(out=sm.t[:, jj * 2:jj * 2 + 1], in_=hr[:, ecol:ecol + 1]), [W1], [sm])
                        self.v(lambda e: e.tensor_copy(out=sm.t[:, jj * 2 + 1:jj * 2 + 2], in_=hi[:, ecol:ecol + 1]), [W1], [sm])
                        self.mm(py, py.t[:, 0:TW], PL, lmat(2, jj), W1, hr, jj == 0, False)
                        self.mm(py, py.t[:, 0:TW], PL, lmat(3, jj), W1, hi, False, jj == 3)
                    self.v(lambda e: e.tensor_tensor(out=Y[:, ts_], in0=Y[:, ts_], in1=py.t[:, 0:TW], op=ALU.add), [PY, py], [PY])
                for jj in range(4):
                    j = fc * 4 + jj
                    self.v(lambda e: e.tensor_copy(out=PP.t[:, STo + (d * 2) * 64 + j:STo + (d * 2) * 64 + j + 1], in_=sm.t[:, jj * 2:jj * 2 + 1]), [sm], [PP])
                    self.v(lambda e: e.tensor_copy(out=PP.t[:, STo + (d * 2 + 1) * 64 + j:STo + (d * 2 + 1) * 64 + j + 1], in_=sm.t[:, jj * 2 + 1:jj * 2 + 2]), [sm], [PP])
            self.gelu(PY, Y, W0, W0.t[:, 0:L], PY, Y)
            self.st(zt_b, zt_b.t[fc * 128:(fc + 1) * 128, :], PY, Y)
        if sor is not None:
            for d in range(2):
                for ri, so_ in ((0, sor), (1, soi)):
                    p = self.ps()
                    o = STo + (d * 2 + ri) * 64
                    self.tr(p, p.t[0:64, 0:128], PP, PP.t[:, o:o + 64])
                    self.v(lambda e: e.tensor_copy(out=sm2.t[0:64, 0:128], in_=p.t[0:64, 0:128]), [p], [sm2])
                    self.st(so_[0], pairv(so_[1][d]), sm2, sm2.t[0:64, 0:128])
        self.linear(L, zt_b, w_glu[0], w_glu[1], 4096, vg_b, lambda t, n0, n1: vg_b.t[t * 128:(t + 1) * 128, n0:n1])

    def dattn(self, L, ht_b, w_qkv, lam_p, subln_g, w_out, layer_idx, kctx, vctx, knew, vnew, qkt_b, vtok_b, ot_b, ymix_b):
        sample = kctx is not None
        wq_b, wq_ap = w_qkv
        self.linear_T(L, ht_b, wq_b, wq_ap[:, 0:4096], 4096, qkt_b)
        if sample:
            vsrc_b, vsrc = vtok_b, vtok_b.t
        else:
            vsrc_b, vsrc = vnew[0], vnew[1]
            kn_b, kn = knew
            self.linear(L, ht_b, wq_b, wq_ap[:, 2048:4096], D, kn_b, lambda t, n0, n1: kn[t * 128:(t + 1) * 128, n0:n1])
        self.linear(L, ht_b, wq_b, wq_ap[:, 4096:6144], D, vsrc_b, lambda t, n0, n1: vsrc[t * 128:(t + 1) * 128, n0:n1])
        KT0, KT1, QT0, QT1, V0, V1, S0, S1, XP = self.pg
        sm, sm2, sm3, smi = self.small, self.small2, self.small3, self.small_i
        lam_init = 0.8 - 0.6 * math.exp(-0.3 * layer_idx)
        scale = 128 ** -0.5
        self.ld(sm2, sm2.t[0:1, 0:512], lam_p[0], lam_p[1].rearrange("a d -> (a d)").rearrange("(o n) -> o n", o=1))
        self.v(lambda e: e.tensor_tensor(out=sm2.t[0:1, 512:640], in0=sm2.t[0:1, 0:128], in1=sm2.t[0:1, 128:256], op=ALU.mult), [sm2], [sm2])
        self.v(lambda e: e.tensor_tensor(out=sm2.t[0:1, 640:768], in0=sm2.t[0:1, 256:384], in1=sm2.t[0:1, 384:512], op=ALU.mult), [sm2], [sm2])
        self.v(lambda e: e.tensor_reduce(out=sm2.t[0:1, 768:770], in_=sm2.t[0:1, 512:768].rearrange("p (a b) -> p a b", a=2), axis=AX.X, op=ALU.add), [sm2], [sm2])
        self.a(lambda e: e.activation(out=sm2.t[0:1, 768:770], in_=sm2.t[0:1, 768:770], func=AF.Exp), [sm2], [sm2])
        self.v(lambda e: e.tensor_tensor(out=sm2.t[0:1, 772:773], in0=sm2.t[0:1, 769:770], in1=sm2.t[0:1, 768:769], op=ALU.subtract), [sm2], [sm2])
        self.v(lambda e: e.tensor_scalar_add(out=sm2.t[0:1, 772:773], in0=sm2.t[0:1, 772:773], scalar1=-lam_init), [sm2], [sm2])
        p = self.ps()
        self.mm(p, p.t[:, 0:1], self.ones_row, self.ones_row.t[0:1, :], sm2, sm2.t[0:1, 772:773], True, True)
        self.v(lambda e: e.tensor_copy(out=sm3.t[:, 0:1], in_=p.t[:, 0:1]), [p], [sm3])
        NL = sm3.t[:, 0:1]
        self.bc_load(sm3, sm3.t[:, 256:512], subln_g[0], subln_g[1].rearrange("(o n) -> o n", o=1))
        self.v(lambda e: e.tensor_scalar(out=sm3.t[:, 256:512], in0=sm3.t[:, 256:512], scalar1=1.0 - lam_init, scalar2=None, op0=ALU.mult), [sm3], [sm3])
        GN = sm3.t[:, 256:512]
        if sample:
            self._rope(L, qkt_b)
        Lc = 256 if sample else 0
        Lk = Lc + L
        NKC = Lk // 128
        V3 = [self.v3(V0, 17, 256), self.v3(V1, 17, 256)]
        vchunk = lambda kc: V3[kc // 17][:, kc % 17, :]
        vbuf = lambda kc: (V0, V1)[kc // 17]
        for h in range(8):
            KT = [KT0, KT1]
            QT = [QT0, QT1]
            for m in range(2):
                c = h * 2 + m
                if sample:
                    for tk in range(2):
                        self.ld(XP, XP.t[:, 0:128], kctx[0], kctx[1][tk * 128:(tk + 1) * 128, h, m, :])
                        p = self.ps()
                        self.tr(p, p.t[:, 0:128], XP, XP.t[:, 0:128])
                        self.a(lambda e: e.copy(out=KT[m].t[:, tk * 128:(tk + 1) * 128], in_=p.t[:, 0:128]), [p], [KT[m]])
                self.ld(KT[m], KT[m].t[:, Lc:Lc + L], qkt_b, qkt_b.t[2048 + c * 128:2048 + (c + 1) * 128, :])
                self.ld(QT[m], QT[m].t[:, 0:L], qkt_b, qkt_b.t[c * 128:(c + 1) * 128, :])
            if sample:
                for tk in range(2):
                    self.ld(vbuf(tk), vchunk(tk), vctx[0], vctx[1][tk * 128:(tk + 1) * 128, h, :])
            for tk in range(L // 128):
                kc = Lc // 128 + tk
                self.ld(vbuf(kc), vchunk(kc), vsrc_b, vsrc[tk * 128:(tk + 1) * 128, h * 256:(h + 1) * 256])
            for qt in range(L // 128):
                qs = slice(qt * 128, (qt + 1) * 128)
                S = [S0, S1]
                for m in range(2):
                    k0 = 0
                    while k0 < Lk:
                        kw = min(512, Lk - k0)
                        p = self.ps()
                        self.mm(p, p.t[:, 0:kw], QT[m], QT[m].t[:, qs], KT[m], KT[m].t[:, k0:k0 + kw], True, True)
                        self.a(lambda e: e.mul(out=S[m].t[:, k0:k0 + kw], in_=p.t[:, 0:kw], mul=scale), [p], [S[m]])
                        k0 += kw
                    st_ = sm.t[:, m * 8:(m + 1) * 8]
                    self.v(lambda e: e.tensor_reduce(out=st_[:, 0:1], in_=S[m].t[:, 0:Lk], axis=AX.X, op=ALU.max), [S[m]], [sm])
                    self.v(lambda e: e.tensor_scalar(out=st_[:, 1:2], in0=st_[:, 0:1], scalar1=-1.0, scalar2=None, op0=ALU.mult), [sm], [sm])
                    self.a(lambda e: e.activation(out=S[m].t[:, 0:Lk], in_=S[m].t[:, 0:Lk], func=AF.Exp, bias=st_[:, 1:2], scale=1.0,
                                                  accum_out=st_[:, 2:3]), [S[m], sm], [S[m], sm])
                    self.v(lambda e: e.reciprocal(out=st_[:, 3:4], in_=st_[:, 2:3]), [sm], [sm])
                self.v(lambda e: e.tensor_tensor(out=sm.t[:, 12:13], in0=sm.t[:, 11:12], in1=NL, op=ALU.mult), [sm, sm3], [sm])
                self.v(lambda e: e.tensor_scalar(out=S0.t[:, 0:Lk], in0=S0.t[:, 0:Lk], scalar1=sm.t[:, 3:4], scalar2=None, op0=ALU.mult), [S0, sm], [S0])
                self.v(lambda e: e.scalar_tensor_tensor(out=S0.t[:, 0:Lk], in0=S1.t[:, 0:Lk], scalar=sm.t[:, 12:13], in1=S0.t[:, 0:Lk],
                                                        op0=ALU.mult, op1=ALU.add), [S1, S0, sm], [S0])
                po = self.ps_held()
                for k4 in range(0, NKC, 4):
                    nk = min(4, NKC - k4)
                    p = self.ps()
                    for i in range(nk):
                        self.tr(p, p.t[:, i * 128:(i + 1) * 128], S0, S0.t[:, (k4 + i) * 128:(k4 + i + 1) * 128])
                    ao = ((k4 // 4) % 4) * 512
                    self.a(lambda e: e.copy(out=XP.t[:, ao:ao + nk * 128], in_=p.t[:, 0:nk * 128]), [p], [XP])
                    for i in range(nk):
                        kc = k4 + i
                        self.mm(po, po.t[:, 0:256], XP, XP.t[:, ao + i * 128:ao + (i + 1) * 128], vbuf(kc), vchunk(kc), kc == 0, kc == NKC - 1)
                ob = sm2
                o = sm2.t[:, 0:256]
                self.v(lambda e: e.tensor_copy(out=o, in_=po.t[:, 0:256]), [po], [sm2])
                self.v(lambda e: e.scalar_tensor_tensor(out=sm2.t[:, 256:512], in0=o, scalar=1.0, in1=o, op0=ALU.mult, op1=ALU.mult,
                                                        accum_out=sm.t[:, 16:17]), [sm2], [sm2, sm])
                self.v(lambda e: e.tensor_scalar(out=sm.t[:, 16:17], in0=sm.t[:, 16:17], scalar1=1.0 / 256.0, scalar2=LN_EPS, op0=ALU.mult, op1=ALU.add), [sm], [sm])
                self.a(lambda e: e.sqrt(out=sm.t[:, 16:17], in_=sm.t[:, 16:17]), [sm], [sm])
                self.v(lambda e: e.reciprocal(out=sm.t[:, 16:17], in_=sm.t[:, 16:17]), [sm], [sm])
                self.v(lambda e: e.scalar_tensor_tensor(out=o, in0=o, scalar=sm.t[:, 16:17], in1=GN, op0=ALU.mult, op1=ALU.mult), [sm2, sm, sm3], [sm2])
                p = self.ps()
                for i in range(2):
                    self.tr(p, p.t[:, i * 128:(i + 1) * 128], sm2, sm2.t[:, i * 128:(i + 1) * 128])
                oo = 2048 + (qt % 4) * 256
                self.a(lambda e: e.copy(out=XP.t[:, oo:oo + 256], in_=p.t[:, 0:256]), [p], [XP])
                self.st(ot_b, ot_b.t[h * 256:(h + 1) * 256, qs].rearrange("(i p) q -> p i q", p=128), XP,
                        XP.t[:, oo:oo + 256].rearrange("p (i q) -> p i q", i=2))
        self.linear(L, ot_b, w_out[0], w_out[1], D, ymix_b, lambda t, n0, n1: ymix_b.t[t * 128:(t + 1) * 128, n0:n1])

    def _rope(self, L, qkt_b):
        COS, SIN, RM, X, T = self.pg[0], self.pg[1], self.pg[2], self.pg[3], self.pg[4]
        sm, smi, smu = self.small, self.small_i, self.small_u
        self.g(lambda e: e.iota(out=smi.t[:, 0:1], pattern=[[0, 1]], base=0, channel_multiplier=1), [], [smi])
        self.v(lambda e: e.tensor_single_scalar(out=smi.t[:, 1:2], in_=smi.t[:, 0:1], scalar=31, op=ALU.bitwise_and), [smi], [smi])
        self.v(lambda e: e.tensor_copy(out=sm.t[:, 0:1], in_=smi.t[:, 1:2]), [smi], [sm])
        self.a(lambda e: e.activation(out=sm.t[:, 1:2], in_=sm.t[:, 0:1], func=AF.Exp, scale=-math.log(10000.0) / 32.0), [sm], [sm])
        self.v(lambda e: e.tensor_scalar(out=sm.t[:, 1:2], in0=sm.t[:, 1:2], scalar1=1.0 / (2.0 * math.pi), scalar2=None, op0=ALU.mult), [sm], [sm])
        Ti = T.t[:, 0:L].bitcast(I32)
        nr = L // GRID_W
        self.g(lambda e: e.iota(out=Ti[0:64, :].rearrange("p (a b) -> p a b", a=nr), pattern=[[1, nr], [0, GRID_W]], base=0, channel_multiplier=0), [], [T])
        self.g(lambda e: e.iota(out=Ti[64:128, :].rearrange("p (a b) -> p a b", a=nr), pattern=[[0, nr], [1, GRID_W]], base=0, channel_multiplier=0), [], [T])
        self.v(lambda e: e.tensor_copy(out=X.t[:, 0:L], in_=Ti), [T], [X])
        self.v(lambda e: e.tensor_scalar(out=X.t[:, 0:L], in0=X.t[:, 0:L], scalar1=sm.t[:, 1:2], scalar2=None, op0=ALU.mult), [X, sm], [X])
        TW = min(512, L)
        for tt in range(L // TW):
            ts_ = slice(tt * TW, (tt + 1) * TW)
            self.sincos(X, X.t[:, ts_], SIN, SIN.t[:, ts_], COS, COS.t[:, ts_], smu, smu.t[:, 0:TW].bitcast(I32), T, T.t[:, ts_])
        rm = RM.t[:, 0:128]
        self.g(lambda e: e.memset(rm, 0.0), [], [RM])
        for (p0, c0, base, fill) in ((0, 32, 0, 1.0), (0, 0, 32, -1.0), (64, 96, 0, 1.0), (64, 64, 32, -1.0)):
            sub = RM.t[p0:p0 + 64, c0:c0 + 32]
            self.g(lambda e: e.affine_select(out=sub, in_=sub, pattern=[[1, 32]], compare_op=ALU.not_equal, fill=fill,
                                             base=base, channel_multiplier=-1), [RM], [RM])
        Xr = Ring([self.pg[5], self.pg[6]])
        for c in range(32):
            xb = Xr.next()
            x = xb.t[:, 0:L]
            self.ld(xb, x, qkt_b, qkt_b.t[c * 128:(c + 1) * 128, :])
            for tt in range(L // TW):
                ts_ = slice(tt * TW, (tt + 1) * TW)
                p = self.ps()
                self.mm(p, p.t[:, 0:TW], RM, rm, xb, x[:, ts_], True, True)
                tmp = self.pg[7].t[:, (tt % 8) * 512:(tt % 8) * 512 + TW]
                self.v(lambda e: e.tensor_tensor(out=tmp, in0=p.t[:, 0:TW], in1=SIN.t[:, ts_], op=ALU.mult), [p, SIN], [self.pg[7]])
                self.v(lambda e: e.tensor_tensor(out=x[:, ts_], in0=x[:, ts_], in1=COS.t[:, ts_], op=ALU.mult), [xb, COS], [xb])
                self.v(lambda e: e.tensor_tensor(out=x[:, ts_], in0=x[:, ts_], in1=tmp, op=ALU.add), [xb, self.pg[7]], [xb])
            self.st(qkt_b, qkt_b.t[c * 128:(c + 1) * 128, :], xb, x)


NCORES = 8
W_SPECS = [
    ("ada_w", [DEPTH, D, 6 * D]), ("ada_b", [DEPTH, 6 * D]), ("ln_g", [DEPTH, 2, D]), ("ln_b", [DEPTH, 2, D]),
    ("rg_w_in", [2, D, 2 * D]), ("rg_conv_w", [2, 4, D]), ("rg_conv_b", [2, D]), ("rg_gate_w", [2, 2, 2, 8, 256, 256]),
    ("rg_gate_b", [2, 2, 2, D]), ("rg_lambda", [2, 2, D]), ("rg_w_out", [2, D, D]),
    ("s5_w_in", [1, D, D]), ("s5_lam_re", [1, 2, 128, 64]), ("s5_lam_im", [1, 2, 128, 64]), ("s5_log_dt", [1, 2, 128]),
    ("s5_b_re", [1, 2, 128, 64, 16]), ("s5_b_im", [1, 2, 128, 64, 16]), ("s5_c_re", [1, 2, 128, 16, 64]), ("s5_c_im", [1, 2, 128, 16, 64]),
    ("s5_d", [1, D]), ("s5_w_glu", [1, D, 2 * D]),
    ("da_w_qkv", [1, D, 3 * D]), ("da_lambda", [1, 4, 128]), ("da_subln_g", [1, 256]), ("da_w_out", [1, D, D]),
    ("peer_wq", [DEPTH, D, D]), ("peer_keys", [DEPTH, 8, 2, 128, 128]),
]


def build_program(ns, npr, ls, lp, depth=DEPTH, past=256):
    nc = bass.Bass("TRN2", target_bir_lowering=False)
    es = ExitStack()
    with es:
        def din(name, shape):
            return Buf(nc.dram_tensor(name, list(shape), F32, kind="ExternalInput").ap(), name)

        def dout(name, shape):
            return Buf(nc.dram_tensor(name, list(shape), F32, kind="ExternalOutput").ap(), name)
        W = {n: din(n, sh) for n, sh in W_SPECS}
        PD = [din(f"pd{l}", [16384, D]) for l in range(depth)]
        PU = [din(f"pu{l}", [16384, D]) for l in range(depth)]
        xs = din("xs", [ns, ls, D])
        xp = din("xp", [npr, lp, D])
        conds = din("conds", [ns + 1, D])
        st_rg = din("st_rg", [ns, 2, 2, D])
        st_sr = din("st_sr", [ns, 1, 2, 128, 64])
        st_si = din("st_si", [ns, 1, 2, 128, 64])
        ck = din("ck", [ns, 1, past, 8, 2, 128])
        cv = din("cv", [ns, 1, past, 8, 256])
        o_yp = dout("o_yp", [npr, lp, D])
        o_ys = dout("o_ys", [ns, ls, D])
        o_rg = dout("o_rg", [npr, 2, 2, D])
        o_sr = dout("o_sr", [npr, 1, 2, 128, 64])
        o_si = dout("o_si", [npr, 1, 2, 128, 64])
        o_k = dout("o_k", [npr, 1, lp, D])
        o_v = dout("o_v", [npr, 1, lp, D])
        k = K(nc, es)
        c = k.c
        LM = max(ls, lp)
        mods = c.dram([depth, ns + 1, 6 * D], name="mods")
        ht = c.dram([D, LM], name="ht")
        hh = c.dram([LM, D], name="hh")
        f1 = c.dram([2 * D, LM], name="f1")
        f2 = c.dram([D, LM], name="f2")
        t1 = c.dram([LM, 2 * D], name="t1")
        t2 = c.dram([LM, D], name="t2")
        xa = c.dram([LM, D], name="xa")
        xb = c.dram([LM, D], name="xb")
        k.mods_phase(conds, ns + 1, W["ada_w"], W["ada_b"], mods, depth)
        T = lambda b: (b, b.t)
        seqs = [("s", i) for i in range(ns)] + [("p", i) for i in range(npr)]
        for kind, si in seqs:
            L = ls if kind == "s" else lp
            ci = si if kind == "s" else ns
            xin = xs if kind == "s" else xp
            xout = o_ys if kind == "s" else o_yp

            def view(b, rows, cols):
                return Buf.__new__(Buf)

            def sub(b, ap):
                nb = Buf(ap, b.name)
                nb.w = b.w
                nb.r = b.r
                return nb
            htv = sub(ht, ht.t[:, 0:L])
            hhv = sub(hh, hh.t[0:L, :])
            f1v = sub(f1, f1.t[:, 0:L])
            f2v = sub(f2, f2.t[:, 0:L])
            t1v = sub(t1, t1.t[0:L, :])
            t1h = sub(t1, t1.t[0:L, 0:D])
            t2v = sub(t2, t2.t[0:L, :])
            row = lambda l, j: (mods, mods.t[l, ci:ci + 1, j * D:(j + 1) * D])
            xbufs = [xa, xb]
            xi = 0
            cur_b, cur_ap = xin, (lambda t: xin.t[si, t * 128:(t + 1) * 128, :])
            k.resid_ln_mod(L, cur_b, cur_ap, None, None, None, None, None, None, row(0, 0), row(0, 1), ht_b=htv)
            for l in range(depth):
                mk, j = l % 3, l // 3
                if mk == 0:
                    wsel = lambda n: (W[n], W[n].t[j])
                    h0 = (st_rg, st_rg.t[si, j]) if kind == "s" else None
                    so = None if kind == "s" else (o_rg, o_rg.t[si, j])
                    k.rglru(L, htv, wsel("rg_w_in"), wsel("rg_conv_w"), wsel("rg_conv_b"), wsel("rg_gate_w"), wsel("rg_gate_b"),
                            wsel("rg_lambda"), wsel("rg_w_out"), h0, so, f1v, f2v, t1h)
                    y_fn = lambda t, Y, ys: k.ld(Y, ys, t1, t1.t[t * 128:(t + 1) * 128, 0:D])
                elif mk == 1:
                    wsel = lambda n: (W[n], W[n].t[j])
                    h0r = (st_sr, st_sr.t[si, j]) if kind == "s" else None
                    h0i = (st_si, st_si.t[si, j]) if kind == "s" else None
                    sor = None if kind == "s" else (o_sr, o_sr.t[si, j])
                    soi = None if kind == "s" else (o_si, o_si.t[si, j])
                    f1h = sub(f1, f1.t[0:D, 0:L])
                    k.s5(L, htv, wsel("s5_w_in"), wsel("s5_lam_re"), wsel("s5_lam_im"), wsel("s5_log_dt"), wsel("s5_b_re"), wsel("s5_b_im"),
                         wsel("s5_c_re"), wsel("s5_c_im"), wsel("s5_d"), wsel("s5_w_glu"), h0r, h0i, sor, soi, f1h, f2v, t1v)

                    def y_fn(t, Y, ys):
                        G = k.pg[7]
                        gs = G.t[:, (t % 2) * D:(t % 2) * D + D]
                        k.ld(Y, ys, t1, t1.t[t * 128:(t + 1) * 128, 0:D])
                        k.ld(G, gs, t1, t1.t[t * 128:(t + 1) * 128, D:2 * D])
                        k.a(lambda e: e.activation(out=gs, in_=gs, func=AF.Sigmoid), [G], [G])
                        k.v(lambda e: e.tensor_tensor(out=ys, in0=ys, in1=gs, op=ALU.mult), [Y, G], [Y])
                else:
                    wsel = lambda n: (W[n], W[n].t[j])
                    kc_ = (ck, ck.t[si, j]) if kind == "s" else None
                    vc_ = (cv, cv.t[si, j]) if kind == "s" else None
                    kn = None if kind == "s" else (o_k, o_k.t[si, j])
                    vn = None if kind == "s" else (o_v, o_v.t[si, j])
                    k.dattn(L, htv, wsel("da_w_qkv"), wsel("da_lambda"), wsel("da_subln_g"), wsel("da_w_out"), l, kc_, vc_, kn, vn,
                            f1v, t2v, f2v, t1h)
                    y_fn = lambda t, Y, ys: k.ld(Y, ys, t1, t1.t[t * 128:(t + 1) * 128, 0:D])
                nxt = xbufs[xi]
                xi ^= 1
                k.resid_ln_mod(L, cur_b, cur_ap, y_fn, row(l, 2), (W["ln_g"], W["ln_g"].t[l, 0:1, :]), (W["ln_b"], W["ln_b"].t[l, 0:1, :]),
                               nxt, (lambda t, nxt=nxt: nxt.t[t * 128:(t + 1) * 128, :]), row(l, 3), row(l, 4), ht_b=htv, h_b=hhv)
                cur_b, cur_ap = nxt, (lambda t, nxt=nxt: nxt.t[t * 128:(t + 1) * 128, :])
                k.peer(L, htv, hhv, W["peer_wq"], W["peer_wq"].t[l], W["peer_keys"], W["peer_keys"].t[l], PD[l], PD[l].t[:, :], PU[l], PU[l].t[:, :],
                       f2v, t1h)
                y_fn = lambda t, Y, ys: k.ld(Y, ys, t1, t1.t[t * 128:(t + 1) * 128, 0:D])
                last = (l == depth - 1)
                nxt = xbufs[xi]
                xi ^= 1
                if last:
                    k.resid_ln_mod(L, cur_b, cur_ap, y_fn, row(l, 5), (W["ln_g"], W["ln_g"].t[l, 1:2, :]), (W["ln_b"], W["ln_b"].t[l, 1:2, :]),
                                   None, None, None, None, final_b=xout, final_ap=(lambda t: xout.t[si, t * 128:(t + 1) * 128, :]))
                else:
                    k.resid_ln_mod(L, cur_b, cur_ap, y_fn, row(l, 5), (W["ln_g"], W["ln_g"].t[l, 1:2, :]), (W["ln_b"], W["ln_b"].t[l, 1:2, :]),
                                   nxt, (lambda t, nxt=nxt: nxt.t[t * 128:(t + 1) * 128, :]), row(l + 1, 0), row(l + 1, 1), ht_b=htv)
                    cur_b, cur_ap = nxt, (lambda t, nxt=nxt: nxt.t[t * 128:(t + 1) * 128, :])
        k.s.finish([o_yp, o_ys, o_rg, o_sr, o_si, o_k, o_v])
        info = (k.s.ninst, k.s.nsem)
    return nc, info


def make_in_maps(inp, ncores, ns, npr):
    f = lambda a: np.ascontiguousarray(np.asarray(a, dtype=np.float32))
    shared = {n: f(inp[n]) for n, _ in W_SPECS}
    for l in range(DEPTH):
        shared[f"pd{l}"] = f(inp["peer_down"][l])
        shared[f"pu{l}"] = f(inp["peer_up"][l])
    maps = []
    for i in range(ncores):
        m = dict(shared)
        sb = slice(i * ns, (i + 1) * ns)
        pb = slice(i * npr, (i + 1) * npr)
        m["xs"] = f(inp["x_sample"][sb])
        m["xp"] = f(inp["x_prompt"][pb])
        m["conds"] = f(np.concatenate([np.asarray(inp["c"])[sb], np.asarray(inp["c_ctx"])[None, :]], axis=0))
        m["st_rg"] = f(inp["state_rglru"][sb])
        m["st_sr"] = f(inp["state_s5_re"][sb])
        m["st_si"] = f(inp["state_s5_im"][sb])
        m["ck"] = f(inp["cache_dattn_k"][sb])
        m["cv"] = f(inp["cache_dattn_v"][sb])
        maps.append(m)
    return maps


def kernel(**inputs):
    nb_s = np.asarray(inputs["x_sample"]).shape[0]
    nb_p = np.asarray(inputs["x_prompt"]).shape[0]
    ls = np.asarray(inputs["x_sample"]).shape[1]
    lp = np.asarray(inputs["x_prompt"]).shape[1]
    ncores = NCORES
    ns, npr = nb_s // ncores, nb_p // ncores
    nc, info = build_program(ns, npr, ls, lp)
    maps = make_in_maps(inputs, ncores, ns, npr)
    res = run_bass_kernel_spmd(nc, maps, core_ids=list(range(ncores)))
    r = res.results
    cat = lambda n: np.concatenate([np.asarray(r[i][n]) for i in range(ncores)], axis=0)
    y_prompt = cat("o_yp")
    y_sample = cat("o_ys")
    rg = cat("o_rg")
    sr = cat("o_sr")
    si = cat("o_si")
    kk = cat("o_k").reshape(nb_p, 1, lp, 8, 2, 128)
    vv = cat("o_v").reshape(nb_p, 1, lp, 8, 256)
    return (y_prompt, y_sample, rg, sr, si, kk, vv)
```

```python
import math
from contextlib import ExitStack
import numpy as np
import concourse.bass as bass
import concourse.mybir as mybir
from concourse.bass_utils import run_bass_kernel_spmd

F32 = mybir.dt.float32
I32 = mybir.dt.int32
U32 = mybir.dt.uint32
ALU = mybir.AluOpType
AF = mybir.ActivationFunctionType
AX = mybir.AxisListType

D = 2048
KC = D // 128
DEPTH = 4
ALPHA = (2 * DEPTH) ** 0.25
LN_EPS = 1e-5
GRID_W = 64
SEM_LIMIT = 30000


class Buf:
    def __init__(self, t, name=""):
        self.t = t
        self.name = name
        self.w = {}
        self.r = {}

    def __getitem__(self, k):
        return self.t[k]


class Sched:
    def __init__(self, nc, es):
        self.nc = nc
        self.es = es
        self.E = {"pe": nc.tensor, "act": nc.scalar, "dve": nc.vector, "pool": nc.gpsimd, "sp": nc.sync}
        self.sems = {}
        self.nsem = 0
        self.cur = {}
        self.seen = {e: {} for e in self.E}
        self.dq = {}
        self.dqi = {}
        self.ninst = 0
        for e in ("pe", "act", "dve", "pool"):
            self._new_engine_sem(e)

    def _alloc(self, name):
        self.nsem += 1
        key = f"{name}_{self.nsem}"
        self.sems[key] = self.es.enter_context(self.nc.semaphore(key))
        return key

    def _new_engine_sem(self, e):
        self.cur[e] = [self._alloc("c" + e), 0]

    def _waits(self, e, reads, writes, skip_same_pe=False, disjoint=False, skip_prefix=None):
        need = {}
        for b in reads:
            for k, v in b.w.items():
                if need.get(k, 0) < v:
                    need[k] = v
        for b in writes:
            if not disjoint:
                for k, v in b.w.items():
                    if need.get(k, 0) < v:
                        need[k] = v
            for k, v in b.r.items():
                if need.get(k, 0) < v:
                    need[k] = v
        eng = self.E[e]
        seen = self.seen[e]
        for k, v in need.items():
            if skip_same_pe and k.startswith("cpe"):
                continue
            if skip_prefix is not None and k.startswith(skip_prefix):
                continue
            if seen.get(k, 0) >= v:
                continue
            eng.wait_ge(self.sems[k], v)
            seen[k] = v

    def _record(self, ev, reads, writes):
        k, v = ev
        for b in reads:
            if b.r.get(k, 0) < v:
                b.r[k] = v
        for b in writes:
            if b.w.get(k, 0) < v:
                b.w[k] = v

    def op(self, e, fn, reads=(), writes=(), accum=False, relax_self=False):
        self._waits(e, reads, writes, skip_same_pe=(e == "pe"), skip_prefix=("c" + e + "_") if relax_self else None)
        inst = fn(self.E[e])
        cur = self.cur[e]
        cur[1] += 1
        inst.then_inc(self.sems[cur[0]], 1)
        ev = (cur[0], cur[1])
        if accum:
            pass
        self._record(ev, reads, writes)
        self.ninst += 1
        if cur[1] >= SEM_LIMIT:
            self._new_engine_sem(e)
        return ev

    def dma(self, e, fn, reads=(), writes=(), nq=8, disjoint=False):
        if e not in self.dq:
            self.dq[e] = [[self._alloc("d" + e), 0] for _ in range(nq)]
            self.dqi[e] = 0
        q = self.dq[e]
        i = self.dqi[e]
        self.dqi[e] = (i + 1) % len(q)
        slot = q[i]
        if slot[1] + 16 > SEM_LIMIT:
            slot[0] = self._alloc("d" + e)
            slot[1] = 0
        eng = self.E[e]
        seen = self.seen[e]
        if slot[1] > 0 and seen.get(slot[0], 0) < slot[1]:
            eng.wait_ge(self.sems[slot[0]], slot[1])
            seen[slot[0]] = slot[1]
        self._waits(e, reads, writes, disjoint=disjoint)
        inst = fn(eng)
        slot[1] += 16
        inst.then_inc(self.sems[slot[0]], 16)
        ev = (slot[0], slot[1])
        self._record(ev, reads, writes)
        self.ninst += 1
        return ev

    def finish(self, bufs):
        self._waits("sp", bufs, ())
        eng = self.E["sp"]
        for e, q in self.dq.items():
            for k, v in q:
                if v > 0 and self.seen["sp"].get(k, 0) < v:
                    eng.wait_ge(self.sems[k], v)
                    self.seen["sp"][k] = v


class Ctx:
    def __init__(self, nc, es):
        self.nc = nc
        self.es = es
        self.s = Sched(nc, es)
        self.n = 0

    def sb(self, shape, dt=F32, name=None):
        self.n += 1
        nm = f"{name or 'sb'}_{self.n}"
        return Buf(self.es.enter_context(self.nc.sbuf_tensor(nm, list(shape), dt)), nm)

    def ps(self, shape=(128, 512), dt=F32, name=None):
        self.n += 1
        nm = f"{name or 'ps'}_{self.n}"
        return Buf(self.es.enter_context(self.nc.psum_tensor(nm, list(shape), dt)), nm)

    def dram(self, shape, dt=F32, name=None):
        self.n += 1
        nm = f"{name or 'dr'}_{self.n}"
        return Buf(self.nc.dram_tensor(nm, list(shape), dt).ap(), nm)


class Ring:
    def __init__(self, bufs):
        self.bufs = bufs
        self.i = 0

    def next(self):
        b = self.bufs[self.i]
        self.i = (self.i + 1) % len(self.bufs)
        return b


class K:
    def __init__(self, nc, es):
        self.c = Ctx(nc, es)
        self.nc = nc
        self.s = self.c.s
        c = self.c
        self.pg = [c.sb([128, 4352], F32, f"pg{i}") for i in range(9)]
        self.small = c.sb([128, 1024], F32, "small")
        self.small2 = c.sb([128, 1024], F32, "small2")
        self.small3 = c.sb([128, 512], F32, "small3")
        self.small_u = c.sb([128, 512], U32, "small_u")
        self.small_i = c.sb([128, 256], I32, "small_i")
        self.ident = c.sb([128, 128], F32, "ident")
        self.ones_row = c.sb([1, 128], F32, "ones_row")
        self.psb = [c.ps([128, 512], F32, f"psb{i}") for i in range(6)]
        self.psi = 0
        self.psh = [c.ps([128, 512], F32, f"psh{i}") for i in range(2)]
        self.pshi = 0
        self._mk_ident()

    def ps(self):
        p = self.psb[self.psi]
        self.psi = (self.psi + 1) % len(self.psb)
        return p

    @staticmethod
    def sub(parent, c0, c1):
        b = Buf(parent.t[:, c0:c1], parent.name + f"[{c0}:{c1}]")
        b.w = dict(parent.w)
        b.r = dict(parent.r)
        return b

    @staticmethod
    def merge_back(parent, subs):
        for sb_ in subs:
            for k_, v_ in sb_.w.items():
                if parent.w.get(k_, 0) < v_:
                    parent.w[k_] = v_
            for k_, v_ in sb_.r.items():
                if parent.r.get(k_, 0) < v_:
                    parent.r[k_] = v_

    def ps_held(self):
        p = self.psh[self.pshi]
        self.pshi = (self.pshi + 1) % len(self.psh)
        return p

    def v(self, fn, reads, writes):
        return self.s.op("dve", fn, reads, writes)

    def a(self, fn, reads, writes):
        return self.s.op("act", fn, reads, writes)

    def g(self, fn, reads, writes):
        return self.s.op("pool", fn, reads, writes)

    def mm(self, out_b, out_ap, lhsT_b, lhsT_ap, rhs_b, rhs_ap, start, stop):
        return self.s.op("pe", lambda e: e.matmul(out_ap, lhsT_ap, rhs_ap, start=start, stop=stop),
                         [lhsT_b, rhs_b], [out_b])

    def tr(self, out_b, out_ap, in_b, in_ap):
        return self.s.op("pe", lambda e: e.transpose(out_ap, in_ap, self.ident[0:in_ap.shape[0], 0:in_ap.shape[0]]),
                         [in_b, self.ident], [out_b])

    def ld(self, out_b, out_ap, in_b, in_ap, q="sp"):
        return self.s.dma(q, lambda e: e.dma_start(out=out_ap, in_=in_ap), [in_b], [out_b])

    def st(self, out_b, out_ap, in_b, in_ap, q="act"):
        return self.s.dma(q, lambda e: e.dma_start(out=out_ap, in_=in_ap), [in_b], [out_b], disjoint=True)

    def _mk_ident(self):
        idt = self.ident
        self.g(lambda e: e.memset(idt[:], 0.0), [], [idt])
        self.g(lambda e: e.affine_select(out=idt[:], in_=idt[:], pattern=[[-1, 128]], compare_op=ALU.not_equal,
                                         fill=1.0, base=0, channel_multiplier=1), [idt], [idt])
        orow = self.ones_row
        self.g(lambda e: e.memset(orow[:], 1.0), [], [orow])

    @staticmethod
    def v3(buf, a, b, off=0):
        return buf.t[:, off:off + a * b].rearrange("p (a b) -> p a b", a=a)

    def bc_load(self, dst, dst_ap, src_b, src_row_ap):
        n = dst_ap.shape[-1]
        self.s.dma("sp", lambda e: e.dma_start(out=dst_ap, in_=src_row_ap.to_broadcast([128, n])), [src_b], [dst])

    def mods_phase(self, conds, ncond, ada_w, ada_b, mods, depth):
        cs, M0, M1, M2, W0, W1, W2, W3 = self.pg[0], self.pg[1], self.pg[2], self.pg[3], self.pg[4], self.pg[5], self.pg[6], self.pg[7]
        Bt = self.pg[8]
        condT = self.small
        self.ld(cs, cs.t[0:ncond, 0:D], conds, conds.t[:, :])
        self.a(lambda e: e.activation(out=cs.t[0:ncond, 0:D], in_=cs.t[0:ncond, 0:D], func=AF.Silu), [cs], [cs])
        for c in range(KC):
            p = self.ps()
            self.tr(p, p.t[:, 0:ncond], cs, cs.t[0:ncond, c * 128:(c + 1) * 128])
            self.v(lambda e, p=p, c=c: e.tensor_copy(out=condT.t[:, c * ncond:(c + 1) * ncond], in_=p.t[:, 0:ncond]), [p], [condT])
        Ms = [M0, M1, M2]
        wring = Ring([(W0, W1), (W2, W3)])
        for l in range(depth):
            for j in range(24):
                wa, wb = wring.next()
                wsrc = ada_w.t[l].rearrange("(c p) n -> p c n", p=128)
                self.ld(wa, self.v3(wa, 8, 512), ada_w, wsrc[:, 0:8, j * 512:(j + 1) * 512])
                self.ld(wb, self.v3(wb, 8, 512), ada_w, wsrc[:, 8:16, j * 512:(j + 1) * 512])
                p = self.ps()
                for c in range(KC):
                    wt = wa if c < 8 else wb
                    self.mm(p, p.t[0:ncond, :], condT, condT.t[:, c * ncond:(c + 1) * ncond],
                            wt, self.v3(wt, 8, 512)[:, c % 8, :], c == 0, c == KC - 1)
                self.s.dma("sp", lambda e, l=l, j=j: e.dma_start(
                    out=Bt.t[0:ncond, 0:512], in_=ada_b.t[l:l + 1, j * 512:(j + 1) * 512].to_broadcast([ncond, 512])),
                    [ada_b], [Bt])
                m = Ms[j // 8]
                col = (j % 8) * 512
                self.v(lambda e, p=p, m=m, col=col: e.tensor_tensor(out=m.t[0:ncond, col:col + 512], in0=p.t[0:ncond, :],
                                                                     in1=Bt.t[0:ncond, 0:512], op=ALU.add), [p, Bt], [m])
            self.v(lambda e: e.tensor_scalar_add(out=M0.t[0:ncond, 2048:4096], in0=M0.t[0:ncond, 2048:4096], scalar1=1.0), [M0], [M0])
            self.v(lambda e: e.tensor_scalar_add(out=M2.t[0:ncond, 0:2048], in0=M2.t[0:ncond, 0:2048], scalar1=1.0), [M2], [M2])
            for i, m in enumerate(Ms):
                self.s.dma("act", lambda e, i=i, m=m, l=l: e.dma_start(out=mods.t[l, :, i * 4096:(i + 1) * 4096], in_=m.t[0:ncond, 0:4096]),
                           [m], [mods], disjoint=True)

    def load_bc(self, page, off, src_b, row_ap):
        self.bc_load(page, page.t[:, off:off + D], src_b, row_ap)

    def resid_ln_mod(self, L, x_b, x_ap, y_fn, gate_row, g_row, b_row, xo_b, xo_ap, shift_row, scale_row,
                     ht_b=None, h_b=None, final_b=None, final_ap=None):
        BC0, BC1, BC2 = self.pg[0], self.pg[1], self.pg[2]
        X, Y, Hh, Tt = self.pg[3], self.pg[4], self.pg[5], self.pg[6]
        st_ = self.small
        if y_fn is not None:
            self.load_bc(BC0, 0, *gate_row)
            self.load_bc(BC0, D, *g_row)
            self.load_bc(BC1, 0, *b_row)
        if shift_row is not None:
            self.load_bc(BC1, D, *shift_row)
            self.load_bc(BC2, 0, *scale_row)
        for t in range(L // 128):
            half = (t % 2) * D
            xs = X.t[:, half:half + D]
            self.ld(X, xs, x_b, x_ap(t))
            if y_fn is not None:
                ys = Y.t[:, half:half + D]
                y_fn(t, Y, ys)
                self.v(lambda e: e.tensor_tensor(out=ys, in0=ys, in1=BC0.t[:, 0:D], op=ALU.mult), [Y, BC0], [Y])
                self.v(lambda e: e.scalar_tensor_tensor(out=xs, in0=xs, scalar=ALPHA, in1=ys, op0=ALU.mult, op1=ALU.add), [X, Y], [X])
                so = (t % 2) * 64
                for q in range(4):
                    self.v(lambda e, q=q: e.bn_stats(out=st_.t[:, so + q * 6:so + q * 6 + 6], in_=xs[:, q * 512:(q + 1) * 512]), [X], [st_])
                self.v(lambda e: e.bn_aggr(out=st_.t[:, so + 32:so + 34], in_=st_.t[:, so:so + 24]), [st_], [st_])
                self.v(lambda e: e.tensor_scalar_add(out=st_.t[:, so + 34:so + 35], in0=st_.t[:, so + 33:so + 34], scalar1=LN_EPS), [st_], [st_])
                self.a(lambda e: e.sqrt(out=st_.t[:, so + 34:so + 35], in_=st_.t[:, so + 34:so + 35]), [st_], [st_])
                self.v(lambda e: e.reciprocal(out=st_.t[:, so + 34:so + 35], in_=st_.t[:, so + 34:so + 35]), [st_], [st_])
                self.v(lambda e: e.tensor_scalar(out=xs, in0=xs, scalar1=st_.t[:, so + 32:so + 33], scalar2=st_.t[:, so + 34:so + 35],
                                                 op0=ALU.subtract, op1=ALU.mult), [X, st_], [X])
                self.v(lambda e: e.tensor_tensor(out=xs, in0=xs, in1=BC0.t[:, D:2 * D], op=ALU.mult), [X, BC0], [X])
                self.v(lambda e: e.tensor_tensor(out=xs, in0=xs, in1=BC1.t[:, 0:D], op=ALU.add), [X, BC1], [X])
                if xo_b is not None:
                    self.st(xo_b, xo_ap(t), X, xs)
                if final_b is not None:
                    self.st(final_b, final_ap(t), X, xs)
            if shift_row is None:
                continue
            hs = Hh.t[:, half:half + D]
            self.v(lambda e: e.tensor_tensor(out=hs, in0=xs, in1=BC2.t[:, 0:D], op=ALU.mult), [X, BC2], [Hh])
            self.v(lambda e: e.tensor_tensor(out=hs, in0=hs, in1=BC1.t[:, D:2 * D], op=ALU.add), [Hh, BC1], [Hh])
            if h_b is not None:
                self.st(h_b, h_b.t[t * 128:(t + 1) * 128, :], Hh, hs)
            ts_ = Tt.t[:, half:half + D]
            for c4 in range(4):
                p = self.ps()
                for k in range(4):
                    c = c4 * 4 + k
                    self.tr(p, p.t[:, k * 128:(k + 1) * 128], Hh, hs[:, c * 128:(c + 1) * 128])
                self.a(lambda e, p=p, c4=c4: e.copy(out=ts_[:, c4 * 512:(c4 + 1) * 512], in_=p.t[:, :]), [p], [Tt])
            self.st(ht_b, ht_b.t.rearrange("(c p) l -> p c l", p=128)[:, :, t * 128:(t + 1) * 128],
                    Tt, ts_.rearrange("p (c l) -> p c l", c=KC))

    def linear_T(self, L, xt_b, w_b, w_ap, N, yt_b, yt_row0=0, kd=D, evac=None):
        kc = kd // 128
        XA, XB = self.pg[0], self.pg[1]
        Wr = Ring([self.pg[2], self.pg[3]])
        Or = Ring([self.pg[4], self.pg[5]])
        TT = min(512, L)
        xsrc = xt_b.t.rearrange("(c p) l -> p c l", p=128)
        wsrc = w_ap.rearrange("(c p) n -> p c n", p=128)
        for t in range(L // TT):
            hk = (kc + 1) // 2
            self.ld(XA, self.v3(XA, hk, TT), xt_b, xsrc[:, 0:hk, t * TT:(t + 1) * TT])
            if kc > hk:
                self.ld(XB, self.v3(XB, kc - hk, TT), xt_b, xsrc[:, hk:kc, t * TT:(t + 1) * TT])
            for oc in range(N // 128):
                w = Wr.next()
                self.ld(w, self.v3(w, kc, 128), w_b, wsrc[:, :, oc * 128:(oc + 1) * 128])
                p = self.ps()
                for c in range(kc):
                    xb = XA if c < hk else XB
                    xv = self.v3(xb, hk if c < hk else kc - hk, TT)[:, c if c < hk else c - hk, :]
                    self.mm(p, p.t[:, 0:TT], w, self.v3(w, kc, 128)[:, c, :], xb, xv, c == 0, c == kc - 1)
                o = Or.next()
                oi = (oc % 8) * 512
                if evac is None:
                    self.a(lambda e, p=p, o=o, oi=oi: e.copy(out=o.t[:, oi:oi + TT], in_=p.t[:, 0:TT]), [p], [o])
                else:
                    evac(p, o, oi, oc, TT)
                r0 = yt_row0 + oc * 128
                self.st(yt_b, yt_b.t[r0:r0 + 128, t * TT:(t + 1) * TT], o, o.t[:, oi:oi + TT])

    def linear(self, L, xt_b, w_b, w_ap, N, y_b, y_ap, kd=D):
        kc = kd // 128
        WA, WB = self.pg[0], self.pg[1]
        Xr = Ring([self.pg[2], self.pg[3]])
        Or = Ring([self.pg[4], self.pg[5]])
        xsrc = xt_b.t.rearrange("(c p) l -> p c l", p=128)
        wsrc = w_ap.rearrange("(c p) n -> p c n", p=128)
        hk = (kc + 1) // 2
        for nb in range(N // 512):
            self.ld(WA, self.v3(WA, hk, 512), w_b, wsrc[:, 0:hk, nb * 512:(nb + 1) * 512])
            if kc > hk:
                self.ld(WB, self.v3(WB, kc - hk, 512), w_b, wsrc[:, hk:kc, nb * 512:(nb + 1) * 512])
            for t in range(L // 128):
                x = Xr.next()
                self.ld(x, self.v3(x, kc, 128), xt_b, xsrc[:, :, t * 128:(t + 1) * 128])
                p = self.ps()
                for c in range(kc):
                    wb_ = WA if c < hk else WB
                    wv = self.v3(wb_, hk if c < hk else kc - hk, 512)[:, c if c < hk else c - hk, :]
                    self.mm(p, p.t[:, :], x, self.v3(x, kc, 128)[:, c, :], wb_, wv, c == 0, c == kc - 1)
                o = Or.next()
                oi = (t % 8) * 512
                self.a(lambda e, p=p, o=o, oi=oi: e.copy(out=o.t[:, oi:oi + 512], in_=p.t[:, :]), [p], [o])
                self.st(y_b, y_ap(t, nb * 512, (nb + 1) * 512), o, o.t[:, oi:oi + 512])

    def gelu(self, xb, x_ap, tb, t_ap, ob, o_ap):
        self.v(lambda e: e.tensor_tensor(out=t_ap, in0=x_ap, in1=x_ap, op=ALU.mult), [xb], [tb])
        self.v(lambda e: e.tensor_scalar(out=t_ap, in0=t_ap, scalar1=0.044715, scalar2=1.0, op0=ALU.mult, op1=ALU.add), [tb], [tb])
        self.v(lambda e: e.tensor_tensor(out=t_ap, in0=t_ap, in1=x_ap, op=ALU.mult), [tb, xb], [tb])
        self.a(lambda e: e.activation(out=t_ap, in_=t_ap, func=AF.Sigmoid, scale=1.5957691216057308), [tb], [tb])
        self.v(lambda e: e.tensor_tensor(out=o_ap, in0=t_ap, in1=x_ap, op=ALU.mult), [tb, xb], [ob])

    def peer(self, L, ht_b, h_b, wq_b, wq_ap, keys_b, keys_ap, down_b, down_ap, up_b, up_ap, qt_b, z_b):
        self.linear_T(L, ht_b, wq_b, wq_ap, D, qt_b)
        KT, S, S2, HQ = self.pg[0], self.pg[1], self.pg[2], self.pg[3]
        Hh = self.sub(HQ, 0, 2048)
        Q = self.sub(HQ, 2176, 4224)
        Zs = [self.sub(self.pg[4], 0, 2048), self.sub(self.pg[4], 2176, 4224), self.sub(self.pg[5], 0, 2048), self.sub(self.pg[5], 2176, 4224)]
        Us = [self.sub(self.pg[6 + i // 2], (i % 2) * 2176, (i % 2) * 2176 + 2048) for i in range(6)]
        Ur = Ring(Us)
        Z = Zs[0]
        sm, sm2, smu, smi = self.small, self.small2, self.small_u, self.small_i
        ktv = self.v3(KT, 16, 128)
        tmpk = self.v3(KT, 16, 128, off=2048)
        self.ld(KT, tmpk, keys_b, keys_ap.rearrange("h s n k -> n (h s) k"))
        for c in range(16):
            p = self.ps()
            self.tr(p, p.t[:, 0:128], KT, tmpk[:, c, :])
            self.a(lambda e: e.copy(out=ktv[:, c, :], in_=p.t[:, 0:128]), [p], [KT])
        self.g(lambda e: e.iota(out=smi.t[:, 0:256], pattern=[[1, 256]], base=0, channel_multiplier=0), [], [smi])
        self.v(lambda e: e.tensor_copy(out=sm2.t[:, 768:1024], in_=smi.t[:, 0:256]), [smi], [sm2])
        iota_bc = sm2.t[:, 768:1024].unsqueeze(1).to_broadcast([128, 16, 256])
        qsrc = qt_b.t.rearrange("(c p) l -> p c l", p=128)
        Vv = sm.t[:, 0:256].rearrange("p (c k) -> p c k", c=16)
        If = sm.t[:, 256:512].rearrange("p (c k) -> p c k", c=16)
        SC = sm.t[:, 512:640].rearrange("p (h k) -> p h k", h=8)
        G = sm.t[:, 640:768].rearrange("p (h k) -> p h k", h=8)
        PRE = sm.t[:, 768:896]
        IDXf = sm.t[:, 896:1024].rearrange("p (h k) -> p h k", h=8)
        Iu = smu.t[:, 0:256].rearrange("p (c k) -> p c k", c=16)
        POSu = smu.t[:, 256:384].rearrange("p (h k) -> p h k", h=8)
        cand = sm2.t[:, 0:256]
        cidx = sm2.t[:, 256:512]
        POSf = sm2.t[:, 512:640].rearrange("p (h k) -> p h k", h=8)
        misc = sm2.t[:, 640:768]
        for t in range(L // 128):
            qv = self.v3(Q, 16, 128)
            self.ld(Q, qv, qt_b, qsrc[:, :, t * 128:(t + 1) * 128])
            self.ld(Hh, Hh.t[:, 0:D], h_b, h_b.t[t * 128:(t + 1) * 128, :])
            for b4 in range(4):
                p = self.ps()
                for k in range(4):
                    c = b4 * 4 + k
                    self.mm(p, p.t[:, k * 128:(k + 1) * 128], Q, qv[:, c, :], KT, ktv[:, c, :], True, True)
                self.a(lambda e: e.copy(out=S.t[:, b4 * 512:(b4 + 1) * 512], in_=p.t[:, :]), [p], [S])
            for c in range(16):
                sl = S.t[:, c * 128:(c + 1) * 128]
                sl2 = S2.t[:, c * 128:(c + 1) * 128]
                self.v(lambda e: e.max(out=Vv[:, c, 0:8], in_=sl), [S], [sm])
                self.v(lambda e: e.max_index(out=Iu[:, c, 0:8], in_max=Vv[:, c, 0:8], in_values=sl), [S, sm], [smu])
                self.v(lambda e: e.match_replace(out=sl2, in_to_replace=Vv[:, c, 0:8], in_values=sl, imm_value=-1e30), [S, sm], [S2])
                self.v(lambda e: e.max(out=Vv[:, c, 8:16], in_=sl2), [S2], [sm])
                self.v(lambda e: e.max_index(out=Iu[:, c, 8:16], in_max=Vv[:, c, 8:16], in_values=sl2), [S2, sm], [smu])
            self.v(lambda e: e.tensor_copy(out=sm.t[:, 256:512], in_=smu.t[:, 0:256]), [smu], [sm])
            for h in range(8):
                c3 = cand.rearrange("p (i j) -> p i j", i=16)
                x3 = cidx.rearrange("p (i j) -> p i j", i=16)
                v1 = Vv[:, 2 * h, :].unsqueeze(2).to_broadcast([128, 16, 16])
                v2 = Vv[:, 2 * h + 1, :].unsqueeze(1).to_broadcast([128, 16, 16])
                i1 = If[:, 2 * h, :].unsqueeze(2).to_broadcast([128, 16, 16])
                i2 = If[:, 2 * h + 1, :].unsqueeze(1).to_broadcast([128, 16, 16])
                self.v(lambda e: e.tensor_tensor(out=c3, in0=v1, in1=v2, op=ALU.add), [sm], [sm2])
                self.v(lambda e: e.max(out=SC[:, h, 0:8], in_=cand), [sm2], [sm])
                self.v(lambda e: e.max_index(out=POSu[:, h, 0:8], in_max=SC[:, h, 0:8], in_values=cand), [sm2, sm], [smu])
                cand2 = self.small3.t[:, 0:256]
                self.v(lambda e: e.match_replace(out=cand2, in_to_replace=SC[:, h, 0:8], in_values=cand, imm_value=-1e30), [sm2, sm], [self.small3])
                self.v(lambda e: e.max(out=SC[:, h, 8:16], in_=cand2), [self.small3], [sm])
                self.v(lambda e: e.max_index(out=POSu[:, h, 8:16], in_max=SC[:, h, 8:16], in_values=cand2), [self.small3, sm], [smu])
                hi_u = smu.t[:, 384:400]
                lo_u = smu.t[:, 400:416]
                self.v(lambda e: e.tensor_single_scalar(out=hi_u, in_=POSu[:, h, :], scalar=4, op=ALU.logical_shift_right), [smu], [smu])
                self.v(lambda e: e.tensor_single_scalar(out=lo_u, in_=POSu[:, h, :], scalar=15, op=ALU.bitwise_and), [smu], [smu])
                hl = self.small3.t[:, 256:288]
                self.v(lambda e: e.tensor_copy(out=hl, in_=smu.t[:, 384:416]), [smu], [self.small3])
                e1 = S2.t[:, 0:256].rearrange("p (k c) -> p k c", k=16)
                e2 = S2.t[:, 256:512].rearrange("p (k c) -> p k c", k=16)
                io16 = sm2.t[:, 768:784].unsqueeze(1).to_broadcast([128, 16, 16])
                self.v(lambda e: e.tensor_tensor(out=e1, in0=io16, in1=hl[:, 0:16].unsqueeze(2).to_broadcast([128, 16, 16]), op=ALU.is_equal),
                       [sm2, self.small3], [S2])
                self.v(lambda e: e.tensor_tensor(out=e1, in0=e1, in1=If[:, 2 * h, :].unsqueeze(1).to_broadcast([128, 16, 16]), op=ALU.mult), [S2, sm], [S2])
                self.v(lambda e: e.tensor_tensor(out=e2, in0=io16, in1=hl[:, 16:32].unsqueeze(2).to_broadcast([128, 16, 16]), op=ALU.is_equal),
                       [sm2, self.small3], [S2])
                self.v(lambda e: e.tensor_tensor(out=e2, in0=e2, in1=If[:, 2 * h + 1, :].unsqueeze(1).to_broadcast([128, 16, 16]), op=ALU.mult), [S2, sm], [S2])
                ab = self.small3.t[:, 288:320]
                self.v(lambda e: e.tensor_reduce(out=ab, in_=S2.t[:, 0:512].rearrange("p (k c) -> p k c", c=16), axis=AX.X, op=ALU.add), [S2], [self.small3])
                self.v(lambda e: e.scalar_tensor_tensor(out=IDXf[:, h, :], in0=ab[:, 0:16], scalar=128.0, in1=ab[:, 16:32], op0=ALU.mult, op1=ALU.add),
                       [self.small3], [sm])
            self.v(lambda e: e.tensor_tensor(out=G, in0=SC, in1=SC[:, :, 0:1].to_broadcast([128, 8, 16]), op=ALU.subtract), [sm], [sm])
            self.a(lambda e: e.activation(out=sm.t[:, 640:768], in_=sm.t[:, 640:768], func=AF.Exp), [sm], [sm])
            self.v(lambda e: e.tensor_reduce(out=misc[:, 0:8], in_=G, axis=AX.X, op=ALU.add), [sm], [sm2])
            self.v(lambda e: e.reciprocal(out=misc[:, 0:8], in_=misc[:, 0:8]), [sm2], [sm2])
            self.v(lambda e: e.tensor_tensor(out=G, in0=G, in1=misc[:, 0:8].unsqueeze(2).to_broadcast([128, 8, 16]), op=ALU.mult), [sm, sm2], [sm])
            self.v(lambda e: e.tensor_copy(out=smi.t[:, 0:128], in_=sm.t[:, 896:1024]), [sm], [smi])
            for sidx in range(128):
                u = Ur.next()
                self.s.dma("pool", lambda e: e.indirect_dma_start(
                    out=u.t[:, 0:D], out_offset=None, in_=down_ap,
                    in_offset=bass.IndirectOffsetOnAxis(ap=smi.t[:, sidx:sidx + 1], axis=0)), [down_b, smi], [u], nq=8)
                self.s.op("dve", lambda e: e.scalar_tensor_tensor(out=u.t[:, 0:D], in0=u.t[:, 0:D], scalar=1.0, in1=Hh.t[:, 0:D],
                                                                  op0=ALU.mult, op1=ALU.mult, accum_out=PRE[:, sidx:sidx + 1]),
                          [u, Hh], [u, sm], relax_self=(sidx > 0))
            self.gelu(sm, PRE, sm2, misc, sm2, misc)
            self.v(lambda e: e.tensor_tensor(out=misc, in0=misc, in1=sm.t[:, 640:768], op=ALU.mult), [sm2, sm], [sm2])
            for sidx in range(128):
                u = Ur.next()
                za = Zs[sidx % 4]
                self.s.dma("pool", lambda e: e.indirect_dma_start(
                    out=u.t[:, 0:D], out_offset=None, in_=up_ap,
                    in_offset=bass.IndirectOffsetOnAxis(ap=smi.t[:, sidx:sidx + 1], axis=0)), [up_b, smi], [u], nq=8)
                if sidx < 4:
                    self.v(lambda e: e.tensor_scalar(out=za.t[:, 0:D], in0=u.t[:, 0:D], scalar1=misc[:, sidx:sidx + 1], scalar2=None, op0=ALU.mult),
                           [u, sm2], [za])
                else:
                    self.v(lambda e: e.scalar_tensor_tensor(out=za.t[:, 0:D], in0=u.t[:, 0:D], scalar=misc[:, sidx:sidx + 1], in1=za.t[:, 0:D],
                                                            op0=ALU.mult, op1=ALU.add), [u, sm2, za], [za])
            self.v(lambda e: e.tensor_tensor(out=Zs[0].t[:, 0:D], in0=Zs[0].t[:, 0:D], in1=Zs[1].t[:, 0:D], op=ALU.add), [Zs[0], Zs[1]], [Zs[0]])
            self.v(lambda e: e.tensor_tensor(out=Zs[2].t[:, 0:D], in0=Zs[2].t[:, 0:D], in1=Zs[3].t[:, 0:D], op=ALU.add), [Zs[2], Zs[3]], [Zs[2]])
            self.v(lambda e: e.tensor_tensor(out=Zs[0].t[:, 0:D], in0=Zs[0].t[:, 0:D], in1=Zs[2].t[:, 0:D], op=ALU.add), [Zs[0], Zs[2]], [Zs[0]])
            self.st(z_b, z_b.t[t * 128:(t + 1) * 128, :], Z, Z.t[:, 0:D])
        self.merge_back(HQ, [Hh, Q])
        self.merge_back(self.pg[4], Zs[0:2])
        self.merge_back(self.pg[5], Zs[2:4])
        for i in range(6):
            self.merge_back(self.pg[6 + i // 2], [Us[i]])

    def load_cols(self, dst_b, dst_ap, src_b, src_rows_ap, stage_b, stage_ap):
        n = src_rows_ap.shape[0]
        self.ld(stage_b, stage_ap[0:n, 0:128], src_b, src_rows_ap)
        p = self.ps()
        self.tr(p, p.t[:, 0:n], stage_b, stage_ap[0:n, 0:128])
        self.v(lambda e: e.tensor_copy(out=dst_ap, in_=p.t[:, 0:n]), [p], [dst_b])

    def rglru(self, L, ht_b, w_in, conv_w, conv_b, gate_w, gate_b, lam, w_out, h0, state_out, gxt_b, yt_b, ymix_b):
        self.linear_T(L, ht_b, w_in[0], w_in[1], 4096, gxt_b)
        XC = [self.pg[0], self.pg[1]]
        A, BB, HS0, HS1, GBp, GW, TT_ = self.pg[2], self.pg[3], self.pg[4], self.pg[5], self.pg[6], self.pg[7], self.pg[8]
        P = self.small3
        sm, sm2 = self.small, self.small2
        stage = TT_
        Pt = P.t
        self.load_cols(P, Pt[:, 0:64], conv_w[0], conv_w[1].rearrange("j (c p) -> (j c) p", p=128), stage, stage.t)
        self.load_cols(P, Pt[:, 64:80], conv_b[0], conv_b[1].rearrange("(c p) -> c p", p=128), stage, stage.t)
        self.load_cols(P, Pt[:, 80:144], gate_b[0], gate_b[1].rearrange("d g (c p) -> (d g c) p", p=128), stage, stage.t)
        self.load_cols(P, Pt[:, 144:176], lam[0], lam[1].rearrange("d (c p) -> (d c) p", p=128), stage, stage.t)
        if h0 is not None:
            self.load_cols(P, Pt[:, 176:208], h0[0], h0[1].rearrange("d (c p) -> (d c) p", p=128), stage, stage.t)
        else:
            self.v(lambda e: e.memset(Pt[:, 176:208], 0.0), [], [P])
        self.a(lambda e: e.activation(out=Pt[:, 208:240], in_=Pt[:, 144:176], func=AF.Exp, scale=-1.0), [P], [P])
        self.a(lambda e: e.activation(out=Pt[:, 208:240], in_=Pt[:, 208:240], func=AF.Ln, bias=1.0), [P], [P])
        self.v(lambda e: e.tensor_scalar(out=Pt[:, 208:240], in0=Pt[:, 208:240], scalar1=-8.0, scalar2=None, op0=ALU.mult), [P], [P])
        NT = max(1, L // 512)
        TW = min(512, L)
        for n in range(8):
            for oc in range(2):
                cc = 2 * n + oc
                xr = A.t[:, 0:L]
                xc = XC[oc].t[:, 0:L]
                self.ld(A, xr, gxt_b, gxt_b.t[2048 + cc * 128:2048 + (cc + 1) * 128, :])
                self.v(lambda e: e.tensor_scalar(out=xc, in0=xr, scalar1=Pt[:, 32 + cc:33 + cc], scalar2=Pt[:, 64 + cc:65 + cc],
                                                 op0=ALU.mult, op1=ALU.add), [A, P], [XC[oc]])
                self.v(lambda e: e.scalar_tensor_tensor(out=xc[:, 2:L], in0=xr[:, 0:L - 2], scalar=Pt[:, cc:cc + 1], in1=xc[:, 2:L],
                                                        op0=ALU.mult, op1=ALU.add), [A, P, XC[oc]], [XC[oc]])
                self.v(lambda e: e.scalar_tensor_tensor(out=xc[:, 1:L], in0=xr[:, 0:L - 1], scalar=Pt[:, 16 + cc:17 + cc], in1=xc[:, 1:L],
                                                        op0=ALU.mult, op1=ALU.add), [A, P, XC[oc]], [XC[oc]])
                self.v(lambda e: e.scalar_tensor_tensor(out=xc[:, 0:L - 1], in0=xr[:, 1:L], scalar=Pt[:, 48 + cc:49 + cc], in1=xc[:, 0:L - 1],
                                                        op0=ALU.mult, op1=ALU.add), [A, P, XC[oc]], [XC[oc]])
            gwv = GW.t[:, 0:2048].rearrange("p (q kc j) -> p q kc j", q=4, kc=2)
            for d in range(2):
                for g_ in range(2):
                    self.ld(GW, gwv[:, d * 2 + g_], gate_w[0], gate_w[1][d, g_, n].rearrange("(kc p) j -> p kc j", p=128))
            for oc in range(2):
                cc = 2 * n + oc
                for d in range(2):
                    HS = HS0 if d == 0 else HS1
                    for tt in range(NT):
                        ts_ = slice(tt * TW, (tt + 1) * TW)
                        rt = sm.t[:, 0:TW]
                        it = sm.t[:, 512:512 + TW]
                        tmp = sm2.t[:, 0:TW]
                        for g_, dst in ((0, rt), (1, it)):
                            p = self.ps()
                            for kc in range(2):
                                self.mm(p, p.t[:, 0:TW], GW, gwv[:, d * 2 + g_, kc, oc * 128:(oc + 1) * 128],
                                        XC[kc], XC[kc].t[:, ts_], kc == 0, kc == 1)
                            bcol = 80 + (d * 2 + g_) * 16 + cc
                            self.a(lambda e: e.activation(out=dst, in_=p.t[:, 0:TW], func=AF.Sigmoid, bias=Pt[:, bcol:bcol + 1], scale=1.0),
                                   [p, P], [sm])
                        ncol = 208 + d * 16 + cc
                        self.a(lambda e: e.activation(out=A.t[:, ts_], in_=rt, func=AF.Exp, scale=Pt[:, ncol:ncol + 1]), [sm, P], [A])
                        self.v(lambda e: e.tensor_tensor(out=tmp, in0=A.t[:, ts_], in1=A.t[:, ts_], op=ALU.mult), [A], [sm2])
                        self.a(lambda e: e.activation(out=tmp, in_=tmp, func=AF.Sqrt, bias=1.0, scale=-1.0), [sm2], [sm2])
                        self.v(lambda e: e.tensor_tensor(out=tmp, in0=tmp, in1=it, op=ALU.mult), [sm2, sm], [sm2])
                        self.v(lambda e: e.tensor_tensor(out=BB.t[:, ts_], in0=tmp, in1=XC[oc].t[:, ts_], op=ALU.mult), [sm2, XC[oc]], [BB])
                    hcol = 176 + d * 16 + cc
                    if d == 0:
                        self.v(lambda e: e.tensor_tensor_scan(out=HS.t[:, 0:L], data0=A.t[:, 0:L], data1=BB.t[:, 0:L],
                                                              initial=Pt[:, hcol:hcol + 1], op0=ALU.mult, op1=ALU.add), [A, BB, P], [HS])
                        self.v(lambda e: e.tensor_copy(out=Pt[:, 240 + cc:241 + cc], in_=HS.t[:, L - 1:L]), [HS], [P])
                    else:
                        self.v(lambda e: e.tensor_tensor_scan(out=HS.t[:, 0:L][:, ::-1], data0=A.t[:, 0:L][:, ::-1], data1=BB.t[:, 0:L][:, ::-1],
                                                              initial=Pt[:, hcol:hcol + 1], op0=ALU.mult, op1=ALU.add), [A, BB, P], [HS])
                        self.v(lambda e: e.tensor_copy(out=Pt[:, 256 + cc:257 + cc], in_=HS.t[:, 0:1]), [HS], [P])
                gb = GBp.t[:, 0:L]
                self.ld(GBp, gb, gxt_b, gxt_b.t[cc * 128:(cc + 1) * 128, :])
                self.v(lambda e: e.tensor_tensor(out=HS0.t[:, 0:L], in0=HS0.t[:, 0:L], in1=HS1.t[:, 0:L], op=ALU.add), [HS0, HS1], [HS0])
                self.gelu(GBp, gb, TT_, TT_.t[:, 0:L], TT_, TT_.t[:, 0:L])
                self.v(lambda e: e.tensor_tensor(out=HS0.t[:, 0:L], in0=HS0.t[:, 0:L], in1=TT_.t[:, 0:L], op=ALU.mult), [HS0, TT_], [HS0])
                self.st(yt_b, yt_b.t[cc * 128:(cc + 1) * 128, :], HS0, HS0.t[:, 0:L])
        if state_out is not None:
            p = self.ps()
            self.tr(p, p.t[0:32, 0:128], P, Pt[:, 240:272])
            self.v(lambda e: e.tensor_copy(out=sm.t[0:32, 0:128], in_=p.t[0:32, 0:128]), [p], [sm])
            self.st(state_out[0], state_out[1].rearrange("d (c p) -> (d c) p", p=128), sm, sm.t[0:32, 0:128])
        self.linear(L, yt_b, w_out[0], w_out[1], D, ymix_b, lambda t, n0, n1: ymix_b.t[t * 128:(t + 1) * 128, n0:n1])

    def sincos(self, yb, y_ap, sb_, sin_ap, cb_, cos_ap, ib, i_ap, tb, t_ap):
        for off, (db, dst) in ((0.0, (sb_, sin_ap)), (0.25, (cb_, cos_ap))):
            if dst is None:
                continue
            self.v(lambda e: e.tensor_scalar_add(out=t_ap, in0=y_ap, scalar1=off), [yb], [tb])
            self.v(lambda e: e.tensor_copy(out=i_ap, in_=t_ap), [tb], [ib])
            self.v(lambda e: e.tensor_copy(out=dst, in_=i_ap), [ib], [db])
            self.v(lambda e: e.tensor_tensor(out=t_ap, in0=t_ap, in1=dst, op=ALU.subtract), [tb, db], [tb])
            self.v(lambda e: e.scalar_tensor_tensor(out=dst, in0=t_ap, scalar=0.5, in1=t_ap, op0=ALU.is_gt, op1=ALU.subtract), [tb], [db])
            self.v(lambda e: e.scalar_tensor_tensor(out=t_ap, in0=t_ap, scalar=-0.5, in1=dst, op0=ALU.is_lt, op1=ALU.subtract), [tb, db], [tb])
            self.a(lambda e: e.activation(out=dst, in_=t_ap, func=AF.Sin, scale=6.283185), [tb], [db])

    def s5(self, L, ht_b, w_in, lam_re, lam_im, log_dt, b_re, b_im, c_re, c_im, d_skip, w_glu, h0r, h0i, sor, soi,
           ut_b, zt_b, vg_b):
        self.linear_T(L, ht_b, w_in[0], w_in[1], D, ut_b)
        PP, PB, PT, PL, PU, PY, W0, W1, STG = self.pg
        sm, sm2, smi = self.small, self.small2, self.small_i
        TW = min(512, L)
        NT = L // TW
        inv2pi = 1.0 / (2.0 * math.pi)
        SL = {n: i * 64 for i, n in enumerate(["lr", "li", "dt", "mag", "th", "sn", "cs", "ar1", "ai", "den", "zr", "zi", "h0r", "h0i", "t0", "t1"])}

        def pp(d, name, j0=0, j1=64):
            o = d * 1024 + SL[name]
            return PP.t[:, o + j0:o + j1]
        STo = 2048
        DSo = 2304
        IOo = 2560
        self.g(lambda e: e.iota(out=smi.t[:, 0:256], pattern=[[1, 256]], base=1, channel_multiplier=0), [], [smi])
        self.v(lambda e: e.tensor_copy(out=PP.t[:, IOo:IOo + 256], in_=smi.t[:, 0:256]), [smi], [PP])
        self.v(lambda e: e.tensor_scalar_add(out=PP.t[:, IOo + 256:IOo + 512], in0=PP.t[:, IOo:IOo + 256], scalar1=256.0), [PP], [PP])
        self.load_cols(PP, PP.t[:, DSo:DSo + 16], d_skip[0], d_skip[1].rearrange("(c p) -> c p", p=128), STG, STG.t)
        pairv = lambda ap: ap.rearrange("(j g2) p -> j (g2 p)", g2=2)
        for d in range(2):
            self.load_cols(PP, pp(d, "lr"), lam_re[0], pairv(lam_re[1][d]), STG, STG.t)
            self.load_cols(PP, pp(d, "li"), lam_im[0], pairv(lam_im[1][d]), STG, STG.t)
            if h0r is not None:
                self.load_cols(PP, pp(d, "h0r"), h0r[0], pairv(h0r[1][d]), STG, STG.t)
                self.load_cols(PP, pp(d, "h0i"), h0i[0], pairv(h0i[1][d]), STG, STG.t)
            else:
                self.v(lambda e: e.memset(pp(d, "h0r"), 0.0), [], [PP])
                self.v(lambda e: e.memset(pp(d, "h0i"), 0.0), [], [PP])
            self.ld(STG, STG.t[0:1, 0:128], log_dt[0], log_dt[1][d:d + 1, :])
            p = self.ps()
            self.mm(p, p.t[:, 0:128], self.ones_row, self.ones_row.t[0:1, :], STG, STG.t[0:1, 0:128], True, True)
            pv = p.t[:, 0:128].rearrange("q (j g2) -> q j g2", g2=2)
            self.a(lambda e: e.activation(out=pp(d, "dt")[0:64, :], in_=pv[0:64, :, 0], func=AF.Exp), [p], [PP])
            self.a(lambda e: e.activation(out=pp(d, "dt")[64:128, :], in_=pv[64:128, :, 1], func=AF.Exp), [p], [PP])
            self.v(lambda e: e.tensor_tensor(out=pp(d, "mag"), in0=pp(d, "lr"), in1=pp(d, "dt"), op=ALU.mult), [PP], [PP])
            self.a(lambda e: e.activation(out=pp(d, "mag"), in_=pp(d, "mag"), func=AF.Exp), [PP], [PP])
            self.v(lambda e: e.tensor_tensor(out=pp(d, "th"), in0=pp(d, "li"), in1=pp(d, "dt"), op=ALU.mult), [PP], [PP])
            self.v(lambda e: e.tensor_scalar(out=pp(d, "th"), in0=pp(d, "th"), scalar1=inv2pi, scalar2=None, op0=ALU.mult), [PP], [PP])
            self.v(lambda e: e.tensor_copy(out=pp(d, "t1"), in_=pp(d, "th")), [PP], [PP])
            self.sincos(PP, pp(d, "t1"), PP, pp(d, "sn"), PP, pp(d, "cs"), smi, smi.t[:, 0:64], PP, pp(d, "t0"))
            self.v(lambda e: e.tensor_tensor(out=pp(d, "ar1"), in0=pp(d, "mag"), in1=pp(d, "cs"), op=ALU.mult), [PP], [PP])
            self.v(lambda e: e.tensor_scalar_add(out=pp(d, "ar1"), in0=pp(d, "ar1"), scalar1=-1.0), [PP], [PP])
            self.v(lambda e: e.tensor_tensor(out=pp(d, "ai"), in0=pp(d, "mag"), in1=pp(d, "sn"), op=ALU.mult), [PP], [PP])
            self.v(lambda e: e.tensor_tensor(out=pp(d, "den"), in0=pp(d, "lr"), in1=pp(d, "lr"), op=ALU.mult), [PP], [PP])
            self.v(lambda e: e.tensor_tensor(out=pp(d, "t0"), in0=pp(d, "li"), in1=pp(d, "li"), op=ALU.mult), [PP], [PP])
            self.v(lambda e: e.tensor_tensor(out=pp(d, "den"), in0=pp(d, "den"), in1=pp(d, "t0"), op=ALU.add), [PP], [PP])
            self.v(lambda e: e.reciprocal(out=pp(d, "den"), in_=pp(d, "den")), [PP], [PP])
            self.v(lambda e: e.tensor_tensor(out=pp(d, "zr"), in0=pp(d, "ar1"), in1=pp(d, "lr"), op=ALU.mult), [PP], [PP])
            self.v(lambda e: e.tensor_tensor(out=pp(d, "t0"), in0=pp(d, "ai"), in1=pp(d, "li"), op=ALU.mult), [PP], [PP])
            self.v(lambda e: e.tensor_tensor(out=pp(d, "zr"), in0=pp(d, "zr"), in1=pp(d, "t0"), op=ALU.add), [PP], [PP])
            self.v(lambda e: e.tensor_tensor(out=pp(d, "zr"), in0=pp(d, "zr"), in1=pp(d, "den"), op=ALU.mult), [PP], [PP])
            self.v(lambda e: e.tensor_tensor(out=pp(d, "zi"), in0=pp(d, "ai"), in1=pp(d, "lr"), op=ALU.mult), [PP], [PP])
            self.v(lambda e: e.tensor_tensor(out=pp(d, "t0"), in0=pp(d, "ar1"), in1=pp(d, "li"), op=ALU.mult), [PP], [PP])
            self.v(lambda e: e.tensor_tensor(out=pp(d, "zi"), in0=pp(d, "zi"), in1=pp(d, "t0"), op=ALU.subtract), [PP], [PP])
            self.v(lambda e: e.tensor_tensor(out=pp(d, "zi"), in0=pp(d, "zi"), in1=pp(d, "den"), op=ALU.mult), [PP], [PP])
            BR = W0.t[:, 0:1024].rearrange("p (j k) -> p j k", k=16)
            BI = W1.t[:, 0:1024].rearrange("p (j k) -> p j k", k=16)
            T1 = W0.t[:, 1024:2048].rearrange("p (j k) -> p j k", k=16)
            T2 = W1.t[:, 1024:2048].rearrange("p (j k) -> p j k", k=16)
            self.ld(W0, BR, b_re[0], b_re[1][d].rearrange("(j g2) p k -> (g2 p) j k", g2=2))
            self.ld(W1, BI, b_im[0], b_im[1][d].rearrange("(j g2) p k -> (g2 p) j k", g2=2))
            zrb = pp(d, "zr").unsqueeze(2).to_broadcast([128, 64, 16])
            zib = pp(d, "zi").unsqueeze(2).to_broadcast([128, 64, 16])
            BBr = PB.t[:, (d * 2) * 1024:(d * 2 + 1) * 1024].rearrange("p (j k) -> p j k", k=16)
            BBi = PB.t[:, (d * 2 + 1) * 1024:(d * 2 + 2) * 1024].rearrange("p (j k) -> p j k", k=16)
            self.v(lambda e: e.tensor_tensor(out=T1, in0=BR, in1=zrb, op=ALU.mult), [W0, PP], [W0])
            self.v(lambda e: e.tensor_tensor(out=T2, in0=BI, in1=zib, op=ALU.mult), [W1, PP], [W1])
            self.v(lambda e: e.tensor_tensor(out=BBr, in0=T1, in1=T2, op=ALU.subtract), [W0, W1], [PB])
            self.v(lambda e: e.tensor_tensor(out=T1, in0=BI, in1=zrb, op=ALU.mult), [W1, PP], [W0])
            self.v(lambda e: e.tensor_tensor(out=T2, in0=BR, in1=zib, op=ALU.mult), [W0, PP], [W1])
            self.v(lambda e: e.tensor_tensor(out=BBi, in0=T1, in1=T2, op=ALU.add), [W0, W1], [PB])
        self.v(lambda e: e.memset(STG.t[:, 0:2048], 0.0), [], [STG])
        zst = lambda kind, jj: STG.t[:, (kind * 4 + jj) * 128:(kind * 4 + jj + 1) * 128]
        lmat = lambda kind, jj: PL.t[:, (kind * 4 + jj) * 128:(kind * 4 + jj + 1) * 128]
        usrc = ut_b.t
        for fc in range(16):
            U = PU.t[:, 0:L]
            Y = PY.t[:, 0:L]
            self.ld(PU, U, ut_b, usrc[fc * 128:(fc + 1) * 128, :])
            self.v(lambda e: e.tensor_scalar(out=Y, in0=U, scalar1=PP.t[:, DSo + fc:DSo + fc + 1], scalar2=None, op0=ALU.mult), [PU, PP], [PY])
            for d in range(2):
                BBr = PB.t[:, (d * 2) * 1024:(d * 2 + 1) * 1024].rearrange("p (j k) -> p j k", k=16)
                BBi = PB.t[:, (d * 2 + 1) * 1024:(d * 2 + 2) * 1024].rearrange("p (j k) -> p j k", k=16)
                for jj in range(4):
                    j = fc * 4 + jj
                    for kind, BBx in ((0, BBr), (1, BBi)):
                        z = zst(kind, jj)
                        for g2 in range(2):
                            co = (2 * jj + g2) * 16
                            self.v(lambda e: e.tensor_copy(out=z[g2 * 64:(g2 + 1) * 64, co:co + 16], in_=BBx[g2 * 64:(g2 + 1) * 64, j, :]), [PB], [STG])
                        p = self.ps()
                        self.tr(p, p.t[:, 0:128], STG, z)
                        self.a(lambda e: e.copy(out=lmat(kind, jj), in_=p.t[:, 0:128]), [p], [PL])
                    for kind, cx in ((2, c_re), (3, c_im)):
                        z = zst(kind, jj)
                        for g2 in range(2):
                            ro = (2 * jj + g2) * 16
                            g = fc * 8 + 2 * jj + g2
                            self.s.dma("sp", lambda e: e.dma_start(out=z[ro:ro + 16, g2 * 64:(g2 + 1) * 64], in_=cx[1][d, g]), [cx[0]], [STG])
                        p = self.ps()
                        self.tr(p, p.t[:, 0:128], STG, z)
                        if kind == 2:
                            self.a(lambda e: e.copy(out=lmat(kind, jj), in_=p.t[:, 0:128]), [p], [PL])
                        else:
                            self.a(lambda e: e.mul(out=lmat(kind, jj), in_=p.t[:, 0:128], mul=-1.0), [p], [PL])
                    Ct = PT.t[:, (jj * 2) * 512:(jj * 2) * 512 + TW]
                    St = PT.t[:, (jj * 2 + 1) * 512:(jj * 2 + 1) * 512 + TW]
                    yv = W0.t[:, 0:TW]
                    self.v(lambda e: e.tensor_scalar(out=yv, in0=PP.t[:, IOo:IOo + TW], scalar1=pp(d, "th", j, j + 1), scalar2=None, op0=ALU.mult), [PP], [W0])
                    self.sincos(W0, yv, PT, St, PT, Ct, self.small_u, self.small_u.t[:, 0:TW].bitcast(I32), W0, W0.t[:, 512:512 + TW])
                for jj in range(4):
                    j = fc * 4 + jj
                    self.v(lambda e: e.tensor_copy(out=sm.t[:, jj * 2:jj * 2 + 1], in_=pp(d, "h0r", j, j + 1)), [PP], [sm])
                    self.v(lambda e: e.tensor_copy(out=sm.t[:, jj * 2 + 1:jj * 2 + 2], in_=pp(d, "h0i", j, j + 1)), [PP], [sm])
                order = range(NT) if d == 0 else range(NT - 1, -1, -1)
                for tt in order:
                    ts_ = slice(tt * TW, (tt + 1) * TW)
                    py = self.ps_held()
                    for jj in range(4):
                        j = fc * 4 + jj
                        Ct = PT.t[:, (jj * 2) * 512:(jj * 2) * 512 + TW]
                        St = PT.t[:, (jj * 2 + 1) * 512:(jj * 2 + 1) * 512 + TW]
                        if d == 1:
                            Ct = Ct[:, ::-1]
                            St = St[:, ::-1]
                        pr = self.ps()
                        pi_ = self.ps()
                        self.mm(pr, pr.t[:, 0:TW], PL, lmat(0, jj), PU, U[:, ts_], True, True)
                        self.mm(pi_, pi_.t[:, 0:TW], PL, lmat(1, jj), PU, U[:, ts_], True, True)
                        wv = lambda b, i: b.t[:, i * 512:i * 512 + TW]
                        t1, t2, kir, kii = wv(W0, 2), wv(W0, 3), wv(W0, 4), wv(W0, 5)
                        kr, ki, hr, hi = wv(W1, 2), wv(W1, 3), wv(W1, 4), wv(W1, 5)
                        self.v(lambda e: e.tensor_tensor(out=t1, in0=pr.t[:, 0:TW], in1=Ct, op=ALU.mult), [pr, PT], [W0])
                        self.v(lambda e: e.tensor_tensor(out=t2, in0=pi_.t[:, 0:TW], in1=St, op=ALU.mult), [pi_, PT], [W0])
                        self.v(lambda e: e.tensor_tensor(out=kir, in0=t1, in1=t2, op=ALU.add), [W0], [W0])
                        self.v(lambda e: e.tensor_tensor(out=t1, in0=pi_.t[:, 0:TW], in1=Ct, op=ALU.mult), [pi_, PT], [W0])
                        self.v(lambda e: e.tensor_tensor(out=t2, in0=pr.t[:, 0:TW], in1=St, op=ALU.mult), [pr, PT], [W0])
                        self.v(lambda e: e.tensor_tensor(out=kii, in0=t1, in1=t2, op=ALU.subtract), [W0], [W0])
                        mdec = pp(d, "mag", j, j + 1).to_broadcast([128, TW])
                        rv = (lambda ap: ap[:, ::-1]) if d == 1 else (lambda ap: ap)
                        self.v(lambda e: e.tensor_tensor_scan(out=rv(kr), data0=mdec, data1=rv(kir), initial=sm.t[:, jj * 2:jj * 2 + 1],
                                                              op0=ALU.mult, op1=ALU.add), [W0, PP, sm], [W1])
                        self.v(lambda e: e.tensor_tensor_scan(out=rv(ki), data0=mdec, data1=rv(kii), initial=sm.t[:, jj * 2 + 1:jj * 2 + 2],
                                                              op0=ALU.mult, op1=ALU.add), [W0, PP, sm], [W1])
                        self.v(lambda e: e.tensor_tensor(out=t1, in0=kr, in1=Ct, op=ALU.mult), [W1, PT], [W0])
                        self.v(lambda e: e.tensor_tensor(out=t2, in0=ki, in1=St, op=ALU.mult), [W1, PT], [W0])
                        self.v(lambda e: e.tensor_tensor(out=hr, in0=t1, in1=t2, op=ALU.subtract), [W0], [W1])
                        self.v(lambda e: e.tensor_tensor(out=t1, in0=ki, in1=Ct, op=ALU.mult), [W1, PT], [W0])
                        self.v(lambda e: e.tensor_tensor(out=t2, in0=kr, in1=St, op=ALU.mult), [W1, PT], [W0])
                        self.v(lambda e: e.tensor_tensor(out=hi, in0=t1, in1=t2, op=ALU.add), [W0], [W1])
                        ecol = TW - 1 if d == 0 else 0
                        self.v(lambda e: e.tensor_copy(out=sm.t[:, jj * 2:jj * 2 + 1], in_=hr[:, ecol:ecol + 1]), [W1], [sm])
                        self.v(lambda e: e.tensor_copy(out=sm.t[:, jj * 2 + 1:jj * 2 + 2], in_=hi[:, ecol:ecol + 1]), [W1], [sm])
                        self.mm(py, py.t[:, 0:TW], PL, lmat(2, jj), W1, hr, jj == 0, False)
                        self.mm(py, py.t[:, 0:TW], PL, lmat(3, jj), W1, hi, False, jj == 3)
                    self.v(lambda e: e.tensor_tensor(out=Y[:, ts_], in0=Y[:, ts_], in1=py.t[:, 0:TW], op=ALU.add), [PY, py], [PY])
                for jj in range(4):
                    j = fc * 4 + jj
                    self.v(lambda e: e.tensor_copy(out=PP.t[:, STo + (d * 2) * 64 + j:STo + (d * 2) * 64 + j + 1], in_=sm.t[:, jj * 2:jj * 2 + 1]), [sm], [PP])
                    self.v(lambda e: e.tensor_copy(out=PP.t[:, STo + (d * 2 + 1) * 64 + j:STo + (d * 2 + 1) * 64 + j + 1], in_=sm.t[:, jj * 2 + 1:jj * 2 + 2]), [sm], [PP])
            self.gelu(PY, Y, W0, W0.t[:, 0:L], PY, Y)
            self.st(zt_b, zt_b.t[fc * 128:(fc + 1) * 128, :], PY, Y)
        if sor is not None:
            for d in range(2):
                for ri, so_ in ((0, sor), (1, soi)):
                    p = self.ps()
                    o = STo + (d * 2 + ri) * 64
                    self.tr(p, p.t[0:64, 0:128], PP, PP.t[:, o:o + 64])
                    self.v(lambda e: e.tensor_copy(out=sm2.t[0:64, 0:128], in_=p.t[0:64, 0:128]), [p], [sm2])
                    self.st(so_[0], pairv(so_[1][d]), sm2, sm2.t[0:64, 0:128])
        self.linear(L, zt_b, w_glu[0], w_glu[1], 4096, vg_b, lambda t, n0, n1: vg_b.t[t * 128:(t + 1) * 128, n0:n1])

    def dattn(self, L, ht_b, w_qkv, lam_p, subln_g, w_out, layer_idx, kctx, vctx, knew, vnew, qkt_b, vtok_b, ot_b, ymix_b):
        sample = kctx is not None
        wq_b, wq_ap = w_qkv
        self.linear_T(L, ht_b, wq_b, wq_ap[:, 0:4096], 4096, qkt_b)
        if sample:
            vsrc_b, vsrc = vtok_b, vtok_b.t
        else:
            vsrc_b, vsrc = vnew[0], vnew[1]
            kn_b, kn = knew
            self.linear(L, ht_b, wq_b, wq_ap[:, 2048:4096], D, kn_b, lambda t, n0, n1: kn[t * 128:(t + 1) * 128, n0:n1])
        self.linear(L, ht_b, wq_b, wq_ap[:, 4096:6144], D, vsrc_b, lambda t, n0, n1: vsrc[t * 128:(t + 1) * 128, n0:n1])
        KT0, KT1, QT0, QT1, V0, V1, S0, S1, XP = self.pg
        sm, sm2, sm3, smi = self.small, self.small2, self.small3, self.small_i
        lam_init = 0.8 - 0.6 * math.exp(-0.3 * layer_idx)
        scale = 128 ** -0.5
        self.ld(sm2, sm2.t[0:1, 0:512], lam_p[0], lam_p[1].rearrange("a d -> (a d)").rearrange("(o n) -> o n", o=1))
        self.v(lambda e: e.tensor_tensor(out=sm2.t[0:1, 512:640], in0=sm2.t[0:1, 0:128], in1=sm2.t[0:1, 128:256], op=ALU.mult), [sm2], [sm2])
        self.v(lambda e: e.tensor_tensor(out=sm2.t[0:1, 640:768], in0=sm2.t[0:1, 256:384], in1=sm2.t[0:1, 384:512], op=ALU.mult), [sm2], [sm2])
        self.v(lambda e: e.tensor_reduce(out=sm2.t[0:1, 768:770], in_=sm2.t[0:1, 512:768].rearrange("p (a b) -> p a b", a=2), axis=AX.X, op=ALU.add), [sm2], [sm2])
        self.a(lambda e: e.activation(out=sm2.t[0:1, 768:770], in_=sm2.t[0:1, 768:770], func=AF.Exp), [sm2], [sm2])
        self.v(lambda e: e.tensor_tensor(out=sm2.t[0:1, 772:773], in0=sm2.t[0:1, 769:770], in1=sm2.t[0:1, 768:769], op=ALU.subtract), [sm2], [sm2])
        self.v(lambda e: e.tensor_scalar_add(out=sm2.t[0:1, 772:773], in0=sm2.t[0:1, 772:773], scalar1=-lam_init), [sm2], [sm2])
        p = self.ps()
        self.mm(p, p.t[:, 0:1], self.ones_row, self.ones_row.t[0:1, :], sm2, sm2.t[0:1, 772:773], True, True)
        self.v(lambda e: e.tensor_copy(out=sm3.t[:, 0:1], in_=p.t[:, 0:1]), [p], [sm3])
        NL = sm3.t[:, 0:1]
        self.bc_load(sm3, sm3.t[:, 256:512], subln_g[0], subln_g[1].rearrange("(o n) -> o n", o=1))
        self.v(lambda e: e.tensor_scalar(out=sm3.t[:, 256:512], in0=sm3.t[:, 256:512], scalar1=1.0 - lam_init, scalar2=None, op0=ALU.mult), [sm3], [sm3])
        GN = sm3.t[:, 256:512]
        if sample:
            self._rope(L, qkt_b)
        Lc = 256 if sample else 0
        Lk = Lc + L
        NKC = Lk // 128
        V3 = [self.v3(V0, 17, 256), self.v3(V1, 17, 256)]
        vchunk = lambda kc: V3[kc // 17][:, kc % 17, :]
        vbuf = lambda kc: (V0, V1)[kc // 17]
        for h in range(8):
            KT = [KT0, KT1]
            QT = [QT0, QT1]
            for m in range(2):
                c = h * 2 + m
                if sample:
                    for tk in range(2):
                        self.ld(XP, XP.t[:, 0:128], kctx[0], kctx[1][tk * 128:(tk + 1) * 128, h, m, :])
                        p = self.ps()
                        self.tr(p, p.t[:, 0:128], XP, XP.t[:, 0:128])
                        self.a(lambda e: e.copy(out=KT[m].t[:, tk * 128:(tk + 1) * 128], in_=p.t[:, 0:128]), [p], [KT[m]])
                self.ld(KT[m], KT[m].t[:, Lc:Lc + L], qkt_b, qkt_b.t[2048 + c * 128:2048 + (c + 1) * 128, :])
                self.ld(QT[m], QT[m].t[:, 0:L], qkt_b, qkt_b.t[c * 128:(c + 1) * 128, :])
            if sample:
                for tk in range(2):
                    self.ld(vbuf(tk), vchunk(tk), vctx[0], vctx[1][tk * 128:(tk + 1) * 128, h, :])
            for tk in range(L // 128):
                kc = Lc // 128 + tk
                self.ld(vbuf(kc), vchunk(kc), vsrc_b, vsrc[tk * 128:(tk + 1) * 128, h * 256:(h + 1) * 256])
            for qt in range(L // 128):
                qs = slice(qt * 128, (qt + 1) * 128)
                S = [S0, S1]
                for m in range(2):
                    k0 = 0
                    while k0 < Lk:
                        kw = min(512, Lk - k0)
                        p = self.ps()
                        self.mm(p, p.t[:, 0:kw], QT[m], QT[m].t[:, qs], KT[m], KT[m].t[:, k0:k0 + kw], True, True)
                        self.a(lambda e: e.mul(out=S[m].t[:, k0:k0 + kw], in_=p.t[:, 0:kw], mul=scale), [p], [S[m]])
                        k0 += kw
                    st_ = sm.t[:, m * 8:(m + 1) * 8]
                    self.v(lambda e: e.tensor_reduce(out=st_[:, 0:1], in_=S[m].t[:, 0:Lk], axis=AX.X, op=ALU.max), [S[m]], [sm])
                    self.v(lambda e: e.tensor_scalar(out=st_[:, 1:2], in0=st_[:, 0:1], scalar1=-1.0, scalar2=None, op0=ALU.mult), [sm], [sm])
                    self.a(lambda e: e.activation(out=S[m].t[:, 0:Lk], in_=S[m].t[:, 0:Lk], func=AF.Exp, bias=st_[:, 1:2], scale=1.0,
                                                  accum_out=st_[:, 2:3]), [S[m], sm], [S[m], sm])
                    self.v(lambda e: e.reciprocal(out=st_[:, 3:4], in_=st_[:, 2:3]), [sm], [sm])
                self.v(lambda e: e.tensor_tensor(out=sm.t[:, 12:13], in0=sm.t[:, 11:12], in1=NL, op=ALU.mult), [sm, sm3], [sm])
                self.v(lambda e: e.tensor_scalar(out=S0.t[:, 0:Lk], in0=S0.t[:, 0:Lk], scalar1=sm.t[:, 3:4], scalar2=None, op0=ALU.mult), [S0, sm], [S0])
                self.v(lambda e: e.scalar_tensor_tensor(out=S0.t[:, 0:Lk], in0=S1.t[:, 0:Lk], scalar=sm.t[:, 12:13], in1=S0.t[:, 0:Lk],
                                                        op0=ALU.mult, op1=ALU.add), [S1, S0, sm], [S0])
                po = self.ps_held()
                for k4 in range(0, NKC, 4):
                    nk = min(4, NKC - k4)
                    p = self.ps()
                    for i in range(nk):
                        self.tr(p, p.t[:, i * 128:(i + 1) * 128], S0, S0.t[:, (k4 + i) * 128:(k4 + i + 1) * 128])
                    ao = ((k4 // 4) % 4) * 512
                    self.a(lambda e: e.copy(out=XP.t[:, ao:ao + nk * 128], in_=p.t[:, 0:nk * 128]), [p], [XP])
                    for i in range(nk):
                        kc = k4 + i
                        self.mm(po, po.t[:, 0:256], XP, XP.t[:, ao + i * 128:ao + (i + 1) * 128], vbuf(kc), vchunk(kc), kc == 0, kc == NKC - 1)
                ob = sm2
                o = sm2.t[:, 0:256]
                self.v(lambda e: e.tensor_copy(out=o, in_=po.t[:, 0:256]), [po], [sm2])
                self.v(lambda e: e.scalar_tensor_tensor(out=sm2.t[:, 256:512], in0=o, scalar=1.0, in1=o, op0=ALU.mult, op1=ALU.mult,
                                                        accum_out=sm.t[:, 16:17]), [sm2], [sm2, sm])
                self.v(lambda e: e.tensor_scalar(out=sm.t[:, 16:17], in0=sm.t[:, 16:17], scalar1=1.0 / 256.0, scalar2=LN_EPS, op0=ALU.mult, op1=ALU.add), [sm], [sm])
                self.a(lambda e: e.sqrt(out=sm.t[:, 16:17], in_=sm.t[:, 16:17]), [sm], [sm])
                self.v(lambda e: e.reciprocal(out=sm.t[:, 16:17], in_=sm.t[:, 16:17]), [sm], [sm])
                self.v(lambda e: e.scalar_tensor_tensor(out=o, in0=o, scalar=sm.t[:, 16:17], in1=GN, op0=ALU.mult, op1=ALU.mult), [sm2, sm, sm3], [sm2])
                p = self.ps()
                for i in range(2):
                    self.tr(p, p.t[:, i * 128:(i + 1) * 128], sm2, sm2.t[:, i * 128:(i + 1) * 128])
                oo = 2048 + (qt % 4) * 256
                self.a(lambda e: e.copy(out=XP.t[:, oo:oo + 256], in_=p.t[:, 0:256]), [p], [XP])
                self.st(ot_b, ot_b.t[h * 256:(h + 1) * 256, qs].rearrange("(i p) q -> p i q", p=128), XP,
                        XP.t[:, oo:oo + 256].rearrange("p (i q) -> p i q", i=2))
        self.linear(L, ot_b, w_out[0], w_out[1], D, ymix_b, lambda t, n0, n1: ymix_b.t[t * 128:(t + 1) * 128, n0:n1])

    def _rope(self, L, qkt_b):
        COS, SIN, RM, X, T = self.pg[0], self.pg[1], self.pg[2], self.pg[3], self.pg[4]
        sm, smi, smu = self.small, self.small_i, self.small_u
        self.g(lambda e: e.iota(out=smi.t[:, 0:1], pattern=[[0, 1]], base=0, channel_multiplier=1), [], [smi])
        self.v(lambda e: e.tensor_single_scalar(out=smi.t[:, 1:2], in_=smi.t[:, 0:1], scalar=31, op=ALU.bitwise_and), [smi], [smi])
        self.v(lambda e: e.tensor_copy(out=sm.t[:, 0:1], in_=smi.t[:, 1:2]), [smi], [sm])
        self.a(lambda e: e.activation(out=sm.t[:, 1:2], in_=sm.t[:, 0:1], func=AF.Exp, scale=-math.log(10000.0) / 32.0), [sm], [sm])
        self.v(lambda e: e.tensor_scalar(out=sm.t[:, 1:2], in0=sm.t[:, 1:2], scalar1=1.0 / (2.0 * math.pi), scalar2=None, op0=ALU.mult), [sm], [sm])
        Ti = T.t[:, 0:L].bitcast(I32)
        nr = L // GRID_W
        self.g(lambda e: e.iota(out=Ti[0:64, :].rearrange("p (a b) -> p a b", a=nr), pattern=[[1, nr], [0, GRID_W]], base=0, channel_multiplier=0), [], [T])
        self.g(lambda e: e.iota(out=Ti[64:128, :].rearrange("p (a b) -> p a b", a=nr), pattern=[[0, nr], [1, GRID_W]], base=0, channel_multiplier=0), [], [T])
        self.v(lambda e: e.tensor_copy(out=X.t[:, 0:L], in_=Ti), [T], [X])
        self.v(lambda e: e.tensor_scalar(out=X.t[:, 0:L], in0=X.t[:, 0:L], scalar1=sm.t[:, 1:2], scalar2=None, op0=ALU.mult), [X, sm], [X])
        TW = min(512, L)
        for tt in range(L // TW):
            ts_ = slice(tt * TW, (tt + 1) * TW)
            self.sincos(X, X.t[:, ts_], SIN, SIN.t[:, ts_], COS, COS.t[:, ts_], smu, smu.t[:, 0:TW].bitcast(I32), T, T.t[:, ts_])
        rm = RM.t[:, 0:128]
        self.g(lambda e: e.memset(rm, 0.0), [], [RM])
        for (p0, c0, base, fill) in ((0, 32, 0, 1.0), (0, 0, 32, -1.0), (64, 96, 0, 1.0), (64, 64, 32, -1.0)):
            sub = RM.t[p0:p0 + 64, c0:c0 + 32]
            self.g(lambda e: e.affine_select(out=sub, in_=sub, pattern=[[1, 32]], compare_op=ALU.not_equal, fill=fill,
                                             base=base, channel_multiplier=-1), [RM], [RM])
        Xr = Ring([self.pg[5], self.pg[6]])
        for c in range(32):
            xb = Xr.next()
            x = xb.t[:, 0:L]
            self.ld(xb, x, qkt_b, qkt_b.t[c * 128:(c + 1) * 128, :])
            for tt in range(L // TW):
                ts_ = slice(tt * TW, (tt + 1) * TW)
                p = self.ps()
                self.mm(p, p.t[:, 0:TW], RM, rm, xb, x[:, ts_], True, True)
                tmp = self.pg[7].t[:, (tt % 8) * 512:(tt % 8) * 512 + TW]
                self.v(lambda e: e.tensor_tensor(out=tmp, in0=p.t[:, 0:TW], in1=SIN.t[:, ts_], op=ALU.mult), [p, SIN], [self.pg[7]])
                self.v(lambda e: e.tensor_tensor(out=x[:, ts_], in0=x[:, ts_], in1=COS.t[:, ts_], op=ALU.mult), [xb, COS], [xb])
                self.v(lambda e: e.tensor_tensor(out=x[:, ts_], in0=x[:, ts_], in1=tmp, op=ALU.add), [xb, self.pg[7]], [xb])
            self.st(qkt_b, qkt_b.t[c * 128:(c + 1) * 128, :], xb, x)


NCORES = 8
W_SPECS = [
    ("ada_w", [DEPTH, D, 6 * D]), ("ada_b", [DEPTH, 6 * D]), ("ln_g", [DEPTH, 2, D]), ("ln_b", [DEPTH, 2, D]),
    ("rg_w_in", [2, D, 2 * D]), ("rg_conv_w", [2, 4, D]), ("rg_conv_b", [2, D]), ("rg_gate_w", [2, 2, 2, 8, 256, 256]),
    ("rg_gate_b", [2, 2, 2, D]), ("rg_lambda", [2, 2, D]), ("rg_w_out", [2, D, D]),
    ("s5_w_in", [1, D, D]), ("s5_lam_re", [1, 2, 128, 64]), ("s5_lam_im", [1, 2, 128, 64]), ("s5_log_dt", [1, 2, 128]),
    ("s5_b_re", [1, 2, 128, 64, 16]), ("s5_b_im", [1, 2, 128, 64, 16]), ("s5_c_re", [1, 2, 128, 16, 64]), ("s5_c_im", [1, 2, 128, 16, 64]),
    ("s5_d", [1, D]), ("s5_w_glu", [1, D, 2 * D]),
    ("da_w_qkv", [1, D, 3 * D]), ("da_lambda", [1, 4, 128]), ("da_subln_g", [1, 256]), ("da_w_out", [1, D, D]),
    ("peer_wq", [DEPTH, D, D]), ("peer_keys", [DEPTH, 8, 2, 128, 128]),
]


def build_program(ns, npr, ls, lp, depth=DEPTH, past=256):
    nc = bass.Bass("TRN2", target_bir_lowering=False)
    es = ExitStack()
    with es:
        def din(name, shape):
            return Buf(nc.dram_tensor(name, list(shape), F32, kind="ExternalInput").ap(), name)

        def dout(name, shape):
            return Buf(nc.dram_tensor(name, list(shape), F32, kind="ExternalOutput").ap(), name)
        W = {n: din(n, sh) for n, sh in W_SPECS}
        PD = [din(f"pd{l}", [16384, D]) for l in range(depth)]
        PU = [din(f"pu{l}", [16384, D]) for l in range(depth)]
        xs = din("xs", [ns, ls, D])
        xp = din("xp", [npr, lp, D])
        conds = din("conds", [ns + 1, D])
        st_rg = din("st_rg", [ns, 2, 2, D])
        st_sr = din("st_sr", [ns, 1, 2, 128, 64])
        st_si = din("st_si", [ns, 1, 2, 128, 64])
        ck = din("ck", [ns, 1, past, 8, 2, 128])
        cv = din("cv", [ns, 1, past, 8, 256])
        o_yp = dout("o_yp", [npr, lp, D])
        o_ys = dout("o_ys", [ns, ls, D])
        o_rg = dout("o_rg", [npr, 2, 2, D])
        o_sr = dout("o_sr", [npr, 1, 2, 128, 64])
        o_si = dout("o_si", [npr, 1, 2, 128, 64])
        o_k = dout("o_k", [npr, 1, lp, D])
        o_v = dout("o_v", [npr, 1, lp, D])
        k = K(nc, es)
        c = k.c
        LM = max(ls, lp)
        mods = c.dram([depth, ns + 1, 6 * D], name="mods")
        ht = c.dram([D, LM], name="ht")
        hh = c.dram([LM, D], name="hh")
        f1 = c.dram([2 * D, LM], name="f1")
        f2 = c.dram([D, LM], name="f2")
        t1 = c.dram([LM, 2 * D], name="t1")
        t2 = c.dram([LM, D], name="t2")
        xa = c.dram([LM, D], name="xa")
        xb = c.dram([LM, D], name="xb")
        k.mods_phase(conds, ns + 1, W["ada_w"], W["ada_b"], mods, depth)
        T = lambda b: (b, b.t)
        seqs = [("s", i) for i in range(ns)] + [("p", i) for i in range(npr)]
        for kind, si in seqs:
            L = ls if kind == "s" else lp
            ci = si if kind == "s" else ns
            xin = xs if kind == "s" else xp
            xout = o_ys if kind == "s" else o_yp

            def view(b, rows, cols):
                return Buf.__new__(Buf)

            def sub(b, ap):
                nb = Buf(ap, b.name)
                nb.w = b.w
                nb.r = b.r
                return nb
            htv = sub(ht, ht.t[:, 0:L])
            hhv = sub(hh, hh.t[0:L, :])
            f1v = sub(f1, f1.t[:, 0:L])
            f2v = sub(f2, f2.t[:, 0:L])
            t1v = sub(t1, t1.t[0:L, :])
            t1h = sub(t1, t1.t[0:L, 0:D])
            t2v = sub(t2, t2.t[0:L, :])
            row = lambda l, j: (mods, mods.t[l, ci:ci + 1, j * D:(j + 1) * D])
            xbufs = [xa, xb]
            xi = 0
            cur_b, cur_ap = xin, (lambda t: xin.t[si, t * 128:(t + 1) * 128, :])
            k.resid_ln_mod(L, cur_b, cur_ap, None, None, None, None, None, None, row(0, 0), row(0, 1), ht_b=htv)
            for l in range(depth):
                mk, j = l % 3, l // 3
                if mk == 0:
                    wsel = lambda n: (W[n], W[n].t[j])
                    h0 = (st_rg, st_rg.t[si, j]) if kind == "s" else None
                    so = None if kind == "s" else (o_rg, o_rg.t[si, j])
                    k.rglru(L, htv, wsel("rg_w_in"), wsel("rg_conv_w"), wsel("rg_conv_b"), wsel("rg_gate_w"), wsel("rg_gate_b"),
                            wsel("rg_lambda"), wsel("rg_w_out"), h0, so, f1v, f2v, t1h)
                    y_fn = lambda t, Y, ys: k.ld(Y, ys, t1, t1.t[t * 128:(t + 1) * 128, 0:D])
                elif mk == 1:
                    wsel = lambda n: (W[n], W[n].t[j])
                    h0r = (st_sr, st_sr.t[si, j]) if kind == "s" else None
                    h0i = (st_si, st_si.t[si, j]) if kind == "s" else None
                    sor = None if kind == "s" else (o_sr, o_sr.t[si, j])
                    soi = None if kind == "s" else (o_si, o_si.t[si, j])
                    f1h = sub(f1, f1.t[0:D, 0:L])
                    k.s5(L, htv, wsel("s5_w_in"), wsel("s5_lam_re"), wsel("s5_lam_im"), wsel("s5_log_dt"), wsel("s5_b_re"), wsel("s5_b_im"),
                         wsel("s5_c_re"), wsel("s5_c_im"), wsel("s5_d"), wsel("s5_w_glu"), h0r, h0i, sor, soi, f1h, f2v, t1v)

                    def y_fn(t, Y, ys):
                        G = k.pg[7]
                        gs = G.t[:, (t % 2) * D:(t % 2) * D + D]
                        k.ld(Y, ys, t1, t1.t[t * 128:(t + 1) * 128, 0:D])
                        k.ld(G, gs, t1, t1.t[t * 128:(t + 1) * 128, D:2 * D])
                        k.a(lambda e: e.activation(out=gs, in_=gs, func=AF.Sigmoid), [G], [G])
                        k.v(lambda e: e.tensor_tensor(out=ys, in0=ys, in1=gs, op=ALU.mult), [Y, G], [Y])
                else:
                    wsel = lambda n: (W[n], W[n].t[j])
                    kc_ = (ck, ck.t[si, j]) if kind == "s" else None
                    vc_ = (cv, cv.t[si, j]) if kind == "s" else None
                    kn = None if kind == "s" else (o_k, o_k.t[si, j])
                    vn = None if kind == "s" else (o_v, o_v.t[si, j])
                    k.dattn(L, htv, wsel("da_w_qkv"), wsel("da_lambda"), wsel("da_subln_g"), wsel("da_w_out"), l, kc_, vc_, kn, vn,
                            f1v, t2v, f2v, t1h)
                    y_fn = lambda t, Y, ys: k.ld(Y, ys, t1, t1.t[t * 128:(t + 1) * 128, 0:D])
                nxt = xbufs[xi]
                xi ^= 1
                k.resid_ln_mod(L, cur_b, cur_ap, y_fn, row(l, 2), (W["ln_g"], W["ln_g"].t[l, 0:1, :]), (W["ln_b"], W["ln_b"].t[l, 0:1, :]),
                               nxt, (lambda t, nxt=nxt: nxt.t[t * 128:(t + 1) * 128, :]), row(l, 3), row(l, 4), ht_b=htv, h_b=hhv)
                cur_b, cur_ap = nxt, (lambda t, nxt=nxt: nxt.t[t * 128:(t + 1) * 128, :])
                k.peer(L, htv, hhv, W["peer_wq"], W["peer_wq"].t[l], W["peer_keys"], W["peer_keys"].t[l], PD[l], PD[l].t[:, :], PU[l], PU[l].t[:, :],
                       f2v, t1h)
                y_fn = lambda t, Y, ys: k.ld(Y, ys, t1, t1.t[t * 128:(t + 1) * 128, 0:D])
                last = (l == depth - 1)
                nxt = xbufs[xi]
                xi ^= 1
                if last:
                    k.resid_ln_mod(L, cur_b, cur_ap, y_fn, row(l, 5), (W["ln_g"], W["ln_g"].t[l, 1:2, :]), (W["ln_b"], W["ln_b"].t[l, 1:2, :]),
                                   None, None, None, None, final_b=xout, final_ap=(lambda t: xout.t[si, t * 128:(t + 1) * 128, :]))
                else:
                    k.resid_ln_mod(L, cur_b, cur_ap, y_fn, row(l, 5), (W["ln_g"], W["ln_g"].t[l, 1:2, :]), (W["ln_b"], W["ln_b"].t[l, 1:2, :]),
                                   nxt, (lambda t, nxt=nxt: nxt.t[t * 128:(t + 1) * 128, :]), row(l + 1, 0), row(l + 1, 1), ht_b=htv)
                    cur_b, cur_ap = nxt, (lambda t, nxt=nxt: nxt.t[t * 128:(t + 1) * 128, :])
        k.s.finish([o_yp, o_ys, o_rg, o_sr, o_si, o_k, o_v])
        info = (k.s.ninst, k.s.nsem)
    return nc, info


def make_in_maps(inp, ncores, ns, npr):
    f = lambda a: np.ascontiguousarray(np.asarray(a, dtype=np.float32))
    shared = {n: f(inp[n]) for n, _ in W_SPECS}
    for l in range(DEPTH):
        shared[f"pd{l}"] = f(inp["peer_down"][l])
        shared[f"pu{l}"] = f(inp["peer_up"][l])
    maps = []
    for i in range(ncores):
        m = dict(shared)
        sb = slice(i * ns, (i + 1) * ns)
        pb = slice(i * npr, (i + 1) * npr)
        m["xs"] = f(inp["x_sample"][sb])
        m["xp"] = f(inp["x_prompt"][pb])
        m["conds"] = f(np.concatenate([np.asarray(inp["c"])[sb], np.asarray(inp["c_ctx"])[None, :]], axis=0))
        m["st_rg"] = f(inp["state_rglru"][sb])
        m["st_sr"] = f(inp["state_s5_re"][sb])
        m["st_si"] = f(inp["state_s5_im"][sb])
        m["ck"] = f(inp["cache_dattn_k"][sb])
        m["cv"] = f(inp["cache_dattn_v"][sb])
        maps.append(m)
    return maps


def kernel(**inputs):
    nb_s = np.asarray(inputs["x_sample"]).shape[0]
    nb_p = np.asarray(inputs["x_prompt"]).shape[0]
    ls = np.asarray(inputs["x_sample"]).shape[1]
    lp = np.asarray(inputs["x_prompt"]).shape[1]
    ncores = NCORES
    ns, npr = nb_s // ncores, nb_p // ncores
    nc, info = build_program(ns, npr, ls, lp)
    maps = make_in_maps(inputs, ncores, ns, npr)
    res = run_bass_kernel_spmd(nc, maps, core_ids=list(range(ncores)))
    r = res.results
    cat = lambda n: np.concatenate([np.asarray(r[i][n]) for i in range(ncores)], axis=0)
    y_prompt = cat("o_yp")
    y_sample = cat("o_ys")
    rg = cat("o_rg")
    sr = cat("o_sr")
    si = cat("o_si")
    kk = cat("o_k").reshape(nb_p, 1, lp, 8, 2, 128)
    vv = cat("o_v").reshape(nb_p, 1, lp, 8, 256)
    return (y_prompt, y_sample, rg, sr, si, kk, vv)
```
